# Optimizing a Trainium2 kernel written in Bass

```python
import math
import jax, jax.numpy as jnp
from jax import lax
import numpy as np

D_MODEL = 2048
BATCH = 16
SEQ = 2048
DEPTH = 2
DEC_BATCH = 16
DEC_SEQ = 32
PAST_LEN = 2048

CHUNK = 64
Q_BLOCK = 128
N_MIXERS = 2
N_RG = (DEPTH + 1) // 2
N_ATTN = DEPTH // 2
D_RNN = D_MODEL
RG_BLOCKS = 16
RG_BLK = D_RNN // RG_BLOCKS
RG_CONV_W = 4
RG_C = 8.0
N_HEADS = 8
HEAD_DIM = D_MODEL // (2 * N_HEADS)
D_ATT = 2 * N_HEADS * HEAD_DIM
D_FF = 5632
FFN_CONV_W = 3
EPS = 1e-6
NEG_INF = -1e30

kernel_name = "hybrid_rglru_diffattn_convffn_stream_step"


def rms_norm(x, g):
    xf = x.astype(jnp.float32)
    y = xf * lax.rsqrt(jnp.mean(xf * xf, axis=-1, keepdims=True) + EPS)
    return (y * g.astype(jnp.float32)).astype(x.dtype)


def causal_dwconv(x_full, w, b, t):
    width = w.shape[0]
    out = b + x_full[:, 0:t] * w[0]
    for k in range(1, width):
        out = out + x_full[:, k:k + t] * w[k]
    return out


def linear_recurrence(a, b, h0):
    b = b.at[:, 0].add(a[:, 0] * h0)

    def combine(e1, e2):
        a1, b1 = e1
        a2, b2 = e2
        return a1 * a2, a2 * b1 + b2

    _, h = lax.associative_scan(combine, (a, b), axis=1)
    return h


def rglru_mixer(x, conv_hist, h0, w_in, conv_w, conv_b, gate_w, gate_b, log_lam, w_out):
    bsz, t, _ = x.shape
    u = x @ w_in
    gate_br, rec = u[..., :D_RNN], u[..., D_RNN:]
    y = jax.nn.gelu(gate_br)
    full = jnp.concatenate([conv_hist.astype(rec.dtype), rec], axis=1)
    xc = causal_dwconv(full, conv_w, conv_b, t)
    new_hist = full[:, -(RG_CONV_W - 1):]
    g = jnp.einsum('btnc,ncd->btnd', xc.reshape(bsz, t, RG_BLOCKS, RG_BLK), gate_w) + gate_b
    g = jax.nn.sigmoid(g.astype(jnp.float32))
    r = g[..., :RG_BLK].reshape(bsz, t, D_RNN)
    i = g[..., RG_BLK:].reshape(bsz, t, D_RNN)
    log_a = -RG_C * r * jax.nn.softplus(-log_lam.astype(jnp.float32))
    a = jnp.exp(log_a)
    mult = jnp.sqrt(-jnp.expm1(2.0 * log_a))
    b = mult * i * xc.astype(jnp.float32)
    h = linear_recurrence(a, b, h0.astype(jnp.float32))
    out = (h.astype(x.dtype) * y) @ w_out
    return out, new_hist, h[:, -1]


def diff_lambda(lam, lambda_init):
    lam = lam.astype(jnp.float32)
    return jnp.exp(jnp.sum(lam[0] * lam[1])) - jnp.exp(jnp.sum(lam[2] * lam[3])) + lambda_init


def diff_qkv(x, w_qkv):
    bsz, t, _ = x.shape
    qkv = x @ w_qkv
    q = qkv[..., :D_ATT].reshape(bsz, t, N_HEADS, 2, HEAD_DIM)
    k = qkv[..., D_ATT:2 * D_ATT].reshape(bsz, t, N_HEADS, 2, HEAD_DIM)
    v = qkv[..., 2 * D_ATT:].reshape(bsz, t, N_HEADS, 2 * HEAD_DIM)
    return q, k, v


def diff_attend(q, k, v, mask, lam):
    s = jnp.einsum('bqhjd,bkhjd->bhjqk', q, k).astype(jnp.float32) * (HEAD_DIM ** -0.5)
    if mask is not None:
        s = jnp.where(mask, s, NEG_INF)
    p = jax.nn.softmax(s, axis=-1)
    attn = p[:, :, 0] - lam * p[:, :, 1]
    return jnp.einsum('bhqk,bkhe->bqhe', attn.astype(v.dtype), v)


def diff_out(o, subln, lambda_init, w_out):
    bsz, t = o.shape[:2]
    o = rms_norm(o, subln) * (1.0 - lambda_init)
    return o.reshape(bsz, t, D_ATT) @ w_out


def diff_attn_prompt(x, w_qkv, lam_p, subln, w_out, lambda_init):
    bsz, t, _ = x.shape
    q, k, v = diff_qkv(x, w_qkv)
    lam = diff_lambda(lam_p, lambda_init)
    n_blk = t // Q_BLOCK
    q_blocks = jnp.moveaxis(q.reshape(bsz, n_blk, Q_BLOCK, N_HEADS, 2, HEAD_DIM), 1, 0)
    key_chunk = jnp.arange(t) // CHUNK

    def one_block(args):
        qb, blk = args
        q_chunk = (blk * Q_BLOCK + jnp.arange(Q_BLOCK)) // CHUNK
        mask = key_chunk[None, :] <= q_chunk[:, None]
        return diff_attend(qb, k, v, mask, lam)

    o = lax.map(one_block, (q_blocks, jnp.arange(n_blk)))
    o = jnp.moveaxis(o, 0, 1).reshape(bsz, t, N_HEADS, 2 * HEAD_DIM)
    return diff_out(o, subln, lambda_init, w_out), k.reshape(bsz, t, 2 * N_HEADS, HEAD_DIM), v


def diff_attn_sample(x, cache_k, cache_v, w_qkv, lam_p, subln, w_out, lambda_init):
    bsz, t, _ = x.shape
    past = cache_k.shape[1]
    q, k, v = diff_qkv(x, w_qkv)
    lam = diff_lambda(lam_p, lambda_init)
    k_all = jnp.concatenate(
        [cache_k.astype(k.dtype).reshape(bsz, past, N_HEADS, 2, HEAD_DIM), k], axis=1)
    v_all = jnp.concatenate([cache_v.astype(v.dtype), v], axis=1)
    o = diff_attend(q, k_all, v_all, None, lam)
    return diff_out(o, subln, lambda_init, w_out), k.reshape(bsz, t, 2 * N_HEADS, HEAD_DIM), v


def conv_ffn(x, hist, w_up, conv_w, conv_b, w_down):
    t = x.shape[1]
    u = x @ w_up
    full = jnp.concatenate([hist.astype(u.dtype), u], axis=1)
    c = causal_dwconv(full, conv_w, conv_b, t)
    h = jax.nn.gelu(c[..., :D_FF]) * c[..., D_FF:]
    return h @ w_down, full[:, -(FFN_CONV_W - 1):]


def setup_inputs(seed: int = 0) -> dict:
    key = jax.random.key(seed)
    ks = jax.random.split(key, 32)
    f32 = jnp.float32
    nrm = lambda k, shape, s: jax.random.normal(k, shape, f32) * s
    u_lam = jax.random.uniform(ks[13], (N_RG, D_RNN), f32, minval=0.9, maxval=0.999)
    return {
        "x_prompt": nrm(ks[0], (BATCH, SEQ, D_MODEL), 1.0),
        "x_sample": nrm(ks[1], (DEC_BATCH, DEC_SEQ, D_MODEL), 1.0),
        "state_rglru_conv": nrm(ks[2], (N_RG, DEC_BATCH, RG_CONV_W - 1, D_RNN), 1.0),
        "state_rglru_h": nrm(ks[3], (N_RG, DEC_BATCH, D_RNN), 0.5),
        "cache_attn_k": nrm(ks[4], (N_ATTN, DEC_BATCH, PAST_LEN, 2 * N_HEADS, HEAD_DIM), 1.0),
        "cache_attn_v": nrm(ks[5], (N_ATTN, DEC_BATCH, PAST_LEN, N_HEADS, 2 * HEAD_DIM), 1.0),
        "state_ffn_conv": nrm(ks[6], (DEPTH, DEC_BATCH, FFN_CONV_W - 1, 2 * D_FF), 1.0),
        "rg_norm": 1.0 + nrm(ks[7], (N_RG, D_MODEL), 0.02),
        "rg_w_in": nrm(ks[8], (N_RG, D_MODEL, 2 * D_RNN), D_MODEL ** -0.5),
        "rg_conv_w": nrm(ks[9], (N_RG, RG_CONV_W, D_RNN), RG_CONV_W ** -0.5),
        "rg_conv_b": nrm(ks[10], (N_RG, D_RNN), 0.01),
        "rg_gate_w": nrm(ks[11], (N_RG, RG_BLOCKS, RG_BLK, 2 * RG_BLK), RG_BLK ** -0.5),
        "rg_gate_b": nrm(ks[12], (N_RG, RG_BLOCKS, 2 * RG_BLK), 0.01),
        "rg_log_lambda": jnp.log(u_lam) - jnp.log1p(-u_lam),
        "rg_w_out": nrm(ks[14], (N_RG, D_RNN, D_MODEL), D_RNN ** -0.5),
        "at_norm": 1.0 + nrm(ks[15], (N_ATTN, D_MODEL), 0.02),
        "at_w_qkv": nrm(ks[16], (N_ATTN, D_MODEL, 3 * D_ATT), D_MODEL ** -0.5),
        "at_lambda": nrm(ks[17], (N_ATTN, 4, HEAD_DIM), 0.1),
        "at_subln": 1.0 + nrm(ks[18], (N_ATTN, 2 * HEAD_DIM), 0.02),
        "at_w_out": nrm(ks[19], (N_ATTN, D_ATT, D_MODEL), D_ATT ** -0.5),
        "ffn_norm": 1.0 + nrm(ks[20], (DEPTH, D_MODEL), 0.02),
        "ffn_w_up": nrm(ks[21], (DEPTH, D_MODEL, 2 * D_FF), D_MODEL ** -0.5),
        "ffn_conv_w": nrm(ks[22], (DEPTH, FFN_CONV_W, 2 * D_FF), FFN_CONV_W ** -0.5),
        "ffn_conv_b": nrm(ks[23], (DEPTH, 2 * D_FF), 0.01),
        "ffn_w_down": nrm(ks[24], (DEPTH, D_FF, D_MODEL), D_FF ** -0.5),
        "final_norm": 1.0 + nrm(ks[25], (D_MODEL,), 0.02),
    }


def reference(x_prompt, x_sample, state_rglru_conv, state_rglru_h, cache_attn_k, cache_attn_v,
              state_ffn_conv, rg_norm, rg_w_in, rg_conv_w, rg_conv_b, rg_gate_w, rg_gate_b,
              rg_log_lambda, rg_w_out, at_norm, at_w_qkv, at_lambda, at_subln, at_w_out,
              ffn_norm, ffn_w_up, ffn_conv_w, ffn_conv_b, ffn_w_down, final_norm):
    xp, xs = x_prompt, x_sample
    bp = xp.shape[0]
    p_rg_conv, p_rg_h, p_k, p_v, p_ffn = [], [], [], [], []
    s_rg_conv, s_rg_h, s_k, s_v, s_ffn = [], [], [], [], []
    for layer in range(DEPTH):
        j = layer // N_MIXERS
        if layer % N_MIXERS == 0:
            w = (rg_w_in[j], rg_conv_w[j], rg_conv_b[j], rg_gate_w[j], rg_gate_b[j],
                 rg_log_lambda[j], rg_w_out[j])
            zero_hist = jnp.zeros((bp, RG_CONV_W - 1, D_RNN), xp.dtype)
            zero_h = jnp.zeros((bp, D_RNN), jnp.float32)
            yp, cp, hp = rglru_mixer(rms_norm(xp, rg_norm[j]), zero_hist, zero_h, *w)
            ys, cs, hs = rglru_mixer(rms_norm(xs, rg_norm[j]), state_rglru_conv[j],
                                     state_rglru_h[j], *w)
            p_rg_conv.append(cp); p_rg_h.append(hp)
            s_rg_conv.append(cs); s_rg_h.append(hs)
        else:
            lambda_init = 0.8 - 0.6 * math.exp(-0.3 * layer)
            yp, kp, vp = diff_attn_prompt(rms_norm(xp, at_norm[j]), at_w_qkv[j], at_lambda[j],
                                          at_subln[j], at_w_out[j], lambda_init)
            ys, kn, vn = diff_attn_sample(rms_norm(xs, at_norm[j]), cache_attn_k[j], cache_attn_v[j],
                                          at_w_qkv[j], at_lambda[j], at_subln[j], at_w_out[j],
                                          lambda_init)
            p_k.append(kp); p_v.append(vp)
            s_k.append(kn); s_v.append(vn)
        xp = xp + yp
        xs = xs + ys
        zero_ffn = jnp.zeros((bp, FFN_CONV_W - 1, 2 * D_FF), xp.dtype)
        fp, fcp = conv_ffn(rms_norm(xp, ffn_norm[layer]), zero_ffn, ffn_w_up[layer],
                           ffn_conv_w[layer], ffn_conv_b[layer], ffn_w_down[layer])
        fs, fcs = conv_ffn(rms_norm(xs, ffn_norm[layer]), state_ffn_conv[layer], ffn_w_up[layer],
                           ffn_conv_w[layer], ffn_conv_b[layer], ffn_w_down[layer])
        p_ffn.append(fcp); s_ffn.append(fcs)
        xp = xp + fp
        xs = xs + fs
    y_prompt = rms_norm(xp, final_norm)
    y_sample = rms_norm(xs, final_norm)
    return (y_prompt, y_sample,
            jnp.stack(p_rg_conv), jnp.stack(p_rg_h), jnp.stack(p_k), jnp.stack(p_v), jnp.stack(p_ffn),
            jnp.stack(s_rg_conv), jnp.stack(s_rg_h), jnp.stack(s_k), jnp.stack(s_v), jnp.stack(s_ffn))
```

```python
import math
import numpy as np
from contextlib import ExitStack
import concourse.bass as bass
import concourse.mybir as mybir
from concourse.bass_utils import run_bass_kernel_spmd

F32 = mybir.dt.float32
BF16 = mybir.dt.bfloat16
AF = mybir.ActivationFunctionType
ALU = mybir.AluOpType

D = 2048
DC = 16
DFF = 5632
FC = 44
NH = 8
HD = 128
TT = 512
EPS = 1e-6
NSEQ = 2
SS = 32
RG_C = 8.0
LAM_INIT = 0.8 - 0.6 * math.exp(-0.3 * 1)
QSCALE = HD ** -0.5
NB = 3
NTMP = 16
NT16 = 8
TW = 520

P_RGN = 0
P_RGCW = 16
P_RGCB = 80
P_RGGB = 96
P_LL = 128
P_ATN = 144
P_LAM = 160
P_SUB = 164
P_FFN = 166
P_FCW = 198
P_FCB = 726
P_FIN = 902
NPRM = 918
S_RGC = 0
S_RGH = 96
S_FFN = 128
NST = 832


class Eng:
    def __init__(self, key, sem):
        self.key = key
        self.sem = sem
        self.n = 0
        self.prog = []
        self.seen = {}


class Buf:
    __slots__ = ("name", "w", "r")

    def __init__(self, name):
        self.name = name
        self.w = {}
        self.r = {}


class FW:
    def __init__(self, nc, stack):
        self.nc = nc
        self.stack = stack
        self.sems = {}
        self.E = {}
        for k in ("pe", "act", "dve", "pool", "sp"):
            self.E[k] = Eng(k, self._sem("eng_" + k))
        self.dsem_n = {}

    def _sem(self, name):
        if name not in self.sems:
            self.sems[name] = self.stack.enter_context(self.nc.semaphore(name))
        return self.sems[name]

    def reset(self):
        for e in self.E.values():
            e.n = 0
            e.prog = []
            e.seen = {}
        self.dsem_n = {}

    def dsem(self, name):
        s = self._sem(name)
        self.dsem_n.setdefault(name, 0)
        return name

    def _wait(self, e, tok):
        kind, obj, val = tok
        if kind == "e":
            if obj is e and val <= e.n - 3:
                return
            if e.seen.get(obj.key, 0) >= val:
                return
            e.seen[obj.key] = val
            sem = obj.sem
        else:
            if e.seen.get(obj, 0) >= val:
                return
            e.seen[obj] = val
            sem = self.sems[obj]
        e.prog.append(lambda eng, sem=sem, val=val: eng.wait_ge(sem, val))

    def _deps(self, e, reads, writes, pwrites):
        for b in reads:
            for t in b.w.values():
                self._wait(e, t)
        for b in writes:
            for t in b.w.values():
                self._wait(e, t)
            for t in b.r.values():
                self._wait(e, t)
        for b in pwrites:
            for t in b.r.values():
                self._wait(e, t)

    def _mark(self, tok, key, reads, writes, pwrites):
        for b in reads:
            b.r[key] = tok
        for b in writes:
            b.w = {key: tok}
            b.r = {}
        for b in pwrites:
            b.w[key] = tok

    def op(self, ek, fn, reads=(), writes=(), pwrites=()):
        e = self.E[ek]
        self._deps(e, reads, writes, pwrites)
        e.n += 1
        tok = ("e", e, e.n)
        e.prog.append(lambda eng, fn=fn, sem=e.sem: fn(eng).then_inc(sem, 1))
        self._mark(tok, e.key, reads, writes, pwrites)
        return tok

    def dma(self, ek, fn, sem, reads=(), writes=(), pwrites=(), n=1):
        e = self.E[ek]
        self._deps(e, reads, writes, pwrites)
        self.dsem_n[sem] += 16 * n
        tok = ("d", sem, self.dsem_n[sem])
        hsem = self.sems[sem]

        def run(eng, fn=fn, hsem=hsem):
            r = fn(eng)
            if not isinstance(r, (list, tuple)):
                r = [r]
            for i in r:
                i.then_inc(hsem, 16)

        e.prog.append(run)
        self._mark(tok, sem, reads, writes, pwrites)
        return tok

    def wait_all(self, ek, bufs):
        e = self.E[ek]
        for b in bufs:
            for t in b.w.values():
                self._wait(e, t)
            for t in b.r.values():
                self._wait(e, t)

    def emit(self):
        nc = self.nc
        with nc.Block() as block:
            @block.tensor
            def _(eng):
                for f in self.E["pe"].prog:
                    f(eng)

            @block.scalar
            def _(eng):
                for f in self.E["act"].prog:
                    f(eng)

            @block.vector
            def _(eng):
                for f in self.E["dve"].prog:
                    f(eng)

            @block.gpsimd
            def _(eng):
                for f in self.E["pool"].prog:
                    f(eng)

            @block.sync
            def _(eng):
                for f in self.E["sp"].prog:
                    f(eng)


class Tmp:
    __slots__ = ("a", "_b", "s", "pool", "k", "gen")

    def __init__(self, t, b, s, pool, k, gen):
        self.a = t
        self._b = b
        self.s = s
        self.pool = pool
        self.k = k
        self.gen = gen

    @property
    def b(self):
        assert self.pool.gen[self.k] == self.gen, "temp reused while live"
        return self._b


class TPool:
    def __init__(self, fw, tensors, name, base=None, idxs=None):
        if base is None:
            self.t = tensors
            self.b = [Buf(f"{name}{i}") for i in range(len(tensors))]
            self.s = [fw.dsem(f"d_{name}{i}") for i in range(len(tensors))]
            self.gen = [0] * len(tensors)
            self.idxs = list(range(len(tensors)))
        else:
            self.t = base.t
            self.b = base.b
            self.s = base.s
            self.gen = base.gen
            self.idxs = list(idxs)
        self.i = 0

    def sub(self, idxs):
        return TPool(None, None, None, base=self, idxs=idxs)

    def get(self):
        k = self.idxs[self.i % len(self.idxs)]
        self.i += 1
        self.gen[k] += 1
        return Tmp(self.t[k], self.b[k], self.s[k], self, k, self.gen[k])


def build_nc(SEQ=2048, PAST=2048, do_sample=True, n_ptiles=None):
    nc = bass.Bass("TRN2", target_bir_lowering=False)
    NTILE = SEQ // TT
    NPB = PAST // 128

    def din(name, shape, dt=F32):
        return nc.dram_tensor(name, list(shape), dt, kind="ExternalInput").ap()

    def dout(name, shape, dt=F32):
        return nc.dram_tensor(name, list(shape), dt, kind="ExternalOutput").ap()

    def dscr(name, shape, dt=BF16):
        return nc.dram_tensor(name, list(shape), dt, kind="Internal").ap()

    xp = din("xp", [NSEQ, SEQ, D])
    xs = din("xs", [NSEQ, SS, D])
    st_in = din("st_in", [128, NST])
    ck = din("ck", [NSEQ, PAST, D])
    cv = din("cv", [NSEQ, PAST, D])
    prm_d = din("prm", [128, NPRM])
    ident_d = din("ident", [128, 128])
    W32 = {
        "rg_in": din("w_rg_in", [D, 2 * D]),
        "gate": din("w_gate", [DC * 128, 256]),
        "rg_out": din("w_rg_out", [D, D]),
        "up0": din("w_up0", [D, 2 * DFF]),
        "down0": din("w_down0", [DFF, D]),
        "qkv": din("w_qkv", [D, 3 * D]),
        "at_out": din("w_at_out", [D, D]),
        "up1": din("w_up1", [D, 2 * DFF]),
        "down1": din("w_down1", [DFF, D]),
    }
    WB = {"gate": dscr("wb_gate", W32["gate"].shape)}
    NSLAB = 96
    WBS = dscr("wbs", [NSLAB, 128, 16 * 512])
    ckb = dscr("ckb", [NSEQ, PAST, D])
    cvb = dscr("cvb", [NSEQ, PAST, D])
    ktT = dscr("ktT", [NSEQ, DC, 128, SEQ])
    vsc = dscr("vsc", [NSEQ, SEQ, D])

    y_p = dout("y_p", [NSEQ, SEQ, D])
    y_s = dout("y_s", [NSEQ, SS, D])
    k_p = dout("k_p", [NSEQ, SEQ, D])
    v_p = dout("v_p", [NSEQ, SEQ, D])
    k_s = dout("k_s", [NSEQ, SS, D])
    v_s = dout("v_s", [NSEQ, SS, D])
    st_p = dout("st_p", [128, NST])
    st_s = dout("st_s", [128, NST])

    stack = ExitStack()
    with stack:
        sb = lambda name, shape, dt: stack.enter_context(nc.sbuf_tensor(name, list(shape), dt))
        xT = sb("xT", [128, DC, TT], F32)
        arena = sb("arena", [128, 64 * 512], BF16)
        wsl = [sb(f"wsl{i}", [128, 16, 512], BF16) for i in range(NB)]
        tmps = [sb(f"tmp{i}", [128, TW], F32) for i in range(NTMP)]
        t16s = [sb(f"t16_{i}", [128, 512], BF16) for i in range(NT16)]
        prm = sb("prm_sb", [128, NPRM], F32)
        dp = sb("dp", [128, 128], F32)
        rstd_sb = sb("rstd_sb", [128, TT], F32)
        stb = sb("stb", [128, NST], F32)
        ident = sb("ident_sb", [128, 128], F32)
        identb = sb("identb", [128, 128], BF16)
        onesb = sb("onesb", [128, 128], BF16)
        psum = [stack.enter_context(nc.psum_tensor(f"ps{i}", [128, 512], F32)) for i in range(8)]
        fw = FW(nc, stack)

        plan = None
        for pass_ in (0, 1):
            fw.reset()
            rec = []
            _emit_all(nc, fw, locals(), plan, rec)
            plan = rec
        fw.emit()
    return nc


def _emit_all(nc, fw, env, plan, rec):
    SEQ = env["SEQ"]; PAST = env["PAST"]; NTILE = env["NTILE"]; NPB = env["NPB"]
    xp = env["xp"]; xs = env["xs"]; st_in = env["st_in"]; ck = env["ck"]; cv = env["cv"]
    prm_d = env["prm_d"]; ident_d = env["ident_d"]; W32 = env["W32"]; WB = env["WB"]; WBS = env["WBS"]; NSLAB = env["NSLAB"]
    ckb = env["ckb"]; cvb = env["cvb"]; ktT = env["ktT"]; vsc = env["vsc"]
    y_p = env["y_p"]; y_s = env["y_s"]; k_p = env["k_p"]; v_p = env["v_p"]; k_s = env["k_s"]; v_s = env["v_s"]
    st_p = env["st_p"]; st_s = env["st_s"]
    xT = env["xT"]; arena = env["arena"]; wsl = env["wsl"]; prm = env["prm"]; dp = env["dp"]; stb = env["stb"]
    ident = env["ident"]; identb = env["identb"]; onesb = env["onesb"]; psum = env["psum"]
    do_sample = env["do_sample"]; n_ptiles = env["n_ptiles"]; rstd_sb = env["rstd_sb"]
    b_rstd = Buf("rstd")

    op = fw.op
    TP = TPool(fw, env["tmps"], "tmp")
    TP16 = TPool(fw, env["t16s"], "t16")
    TP_XC = TP.sub(range(0, 5))
    TP_RB = TP.sub(range(5, 8))
    TP_G = TP.sub(range(8, 16))
    b_xT = [Buf(f"xT{c}") for c in range(DC)]
    b_slot = [Buf(f"slot{i}") for i in range(64)]
    b_ps = [Buf(f"ps{i}") for i in range(8)]
    b_prm = Buf("prm"); b_dp = Buf("dp"); b_const = Buf("const")
    b_rgc = [Buf(f"rgc{c}") for c in range(DC)]
    b_rgh = [Buf(f"rgh{c}") for c in range(DC)]
    b_ffn = [[Buf(f"ffn{l}_{j}") for j in range(2 * FC)] for l in range(2)]
    all_st = b_rgc + b_rgh + b_ffn[0] + b_ffn[1]
    b_wb = {k: Buf("wb_" + k) for k in WB}
    b_wbs = [Buf(f"wbs{i}") for i in range(NSLAB)]
    b_ckb = Buf("ckb"); b_cvb = Buf("cvb")
    b_kt = [Buf(f"ktsc{s}") for s in range(NSEQ)]
    b_vs = [Buf(f"vssc{s}") for s in range(NSEQ)]
    b_wsl = [Buf(f"wsl{i}") for i in range(NB)]
    s_wsl = [fw.dsem(f"d_wsl{i}") for i in range(NB)]
    s_misc = fw.dsem("d_misc")
    s_prm = fw.dsem("d_prm")
    s_gwb = fw.dsem("d_gwb")
    s_ident = fw.dsem("d_ident")
    s_st = fw.dsem("d_st")
    s_kv = [fw.dsem(f"d_kv{i}") for i in range(4)]
    pool_ok = [False]
    si_ = [0]

    def EP():
        return "pool" if pool_ok[0] else "dve"

    def av(off, shape):
        n = int(np.prod(shape))
        v = arena[:, off:off + n]
        if len(shape) == 2:
            v = v.rearrange("p (a b) -> p a b", b=shape[1])
        bufs = b_slot[off // 512:(off + n - 1) // 512 + 1]
        return v, bufs

    def slot(i):
        return arena[:, i * 512:(i + 1) * 512]

    ps_i = [0]

    def ps_next():
        k = ps_i[0] % 8
        ps_i[0] += 1
        return k

    rgc_v = stb[:, S_RGC:S_RGC + 96].rearrange("p (c b k) -> p c b k", b=2, k=3)
    rgh_v = stb[:, S_RGH:S_RGH + 32].rearrange("p (c b) -> p c b", b=2)
    ffn_v = stb[:, S_FFN:S_FFN + 704].rearrange("p (l j b k) -> p l j b k", j=88, b=2, k=2)

    def pcol(off, i=0):
        return prm[:, off + i:off + i + 1]

    fw.dma("sp", lambda e: e.dma_start(out=prm[:], in_=prm_d[:]), s_prm, writes=[b_prm])
    fw.dma("sp", lambda e: e.dma_start(out=ident[:], in_=ident_d[:]), s_ident, writes=[b_const])
    op("dve", lambda e: e.tensor_copy(out=identb[:], in_=ident[:]), reads=[b_const], pwrites=[b_const])
    op("dve", lambda e: e.memset(onesb[:], 1.0), pwrites=[b_const])
    op("act", lambda e: e.activation(out=dp[:, 0:16], in_=prm[:, P_LL:P_LL + 16], func=AF.Exp, scale=-1.0),
       reads=[b_prm], writes=[b_dp])
    op("act", lambda e: e.activation(out=dp[:, 0:16], in_=dp[:, 0:16], func=AF.Ln, bias=1.0), writes=[b_dp])
    op("dve", lambda e: e.tensor_scalar(out=dp[:, 16:32], in0=dp[:, 0:16], scalar1=-0.5 * RG_C, scalar2=None,
                                        op0=ALU.mult), reads=[b_dp], pwrites=[b_dp])
    op("dve", lambda e: e.tensor_scalar(out=dp[:, 0:16], in0=dp[:, 0:16], scalar1=-RG_C, scalar2=None,
                                        op0=ALU.mult), writes=[b_dp])
    op("dve", lambda e: e.tensor_scalar(out=dp[:, 64:96], in0=prm[:, P_RGGB:P_RGGB + 32], scalar1=0.5, scalar2=None,
                                        op0=ALU.mult), reads=[b_prm], writes=[b_dp])
    op("dve", lambda e: e.tensor_scalar(out=dp[:, 32:34], in0=prm[:, P_SUB:P_SUB + 2], scalar1=1.0 - LAM_INIT,
                                        scalar2=None, op0=ALU.mult), reads=[b_prm], writes=[b_dp])
    lt = TP16.get()
    op("dve", lambda e: e.tensor_tensor(out=lt.a[:, 0:1], in0=pcol(P_LAM, 0), in1=pcol(P_LAM, 1), op=ALU.mult),
       reads=[b_prm], writes=[lt.b])
    op("dve", lambda e: e.tensor_tensor(out=lt.a[:, 1:2], in0=pcol(P_LAM, 2), in1=pcol(P_LAM, 3), op=ALU.mult),
       reads=[b_prm], writes=[lt.b])
    k = ps_next()
    op("pe", lambda e: e.matmul(psum[k][:, 0:2], lhsT=onesb[:], rhs=lt.a[:, 0:2], start=True, stop=True),
       reads=[lt.b, b_const], writes=[b_ps[k]])
    op("act", lambda e: e.activation(out=dp[:, 40:42], in_=psum[k][:, 0:2], func=AF.Exp), reads=[b_ps[k]], writes=[b_dp])
    op("dve", lambda e: e.tensor_tensor(out=dp[:, 34:35], in0=dp[:, 41:42], in1=dp[:, 40:41], op=ALU.subtract),
       writes=[b_dp])
    op("dve", lambda e: e.tensor_scalar(out=dp[:, 34:35], in0=dp[:, 34:35], scalar1=-LAM_INIT, scalar2=None,
                                        op0=ALU.add), writes=[b_dp])
    CL = lambda c: dp[:, c:c + 1]
    HCL = lambda c: dp[:, 16 + c:17 + c]
    HGB = lambda i: dp[:, 64 + i:65 + i]
    GS = lambda e_: dp[:, 32 + e_:33 + e_]
    NEGLAM = dp[:, 34:35]

    wi = [0]
    issued = [0]

    cur_first = [True]
    s_wbk = [fw.dsem(f"d_wbk{i}") for i in range(NB)]

    def _issue(j):
        spec, first = plan[j]
        kname, k0, nk, c0, ncol = spec
        idx = j % NSLAB
        assert plan[idx][0] == spec
        src32 = W32[kname][k0 * 128:(k0 + nk) * 128, c0:c0 + ncol].rearrange("(kc p) n -> p kc n", p=128)
        scr = WBS[idx][:, 0:nk * ncol].rearrange("p (kc n) -> p kc n", n=ncol)
        bi = j % NB
        dst = wsl[bi][:, 0:nk, 0:ncol]
        if first:
            fw.dma("pool", lambda e: e.dma_start(out=dst, in_=src32), s_wsl[bi], writes=[b_wsl[bi]])
            fw.dma("sp", lambda e: e.dma_start(out=scr, in_=dst), s_wbk[bi], reads=[b_wsl[bi]], writes=[b_wbs[idx]])
        else:
            fw.dma("sp", lambda e: e.dma_start(out=dst, in_=scr), s_wsl[bi], reads=[b_wbs[idx]], writes=[b_wsl[bi]])

    released = {}
    held = set()

    def _try_issue():
        if plan is None:
            return
        lim = min(wi[0] - 1 + NB, len(plan))
        while issued[0] < lim:
            j = issued[0]
            if j - NB >= 0 and not released.get(j - NB, False):
                break
            _issue(j)
            issued[0] += 1

    def slab(spec, hold=False):
        j = wi[0]
        wi[0] += 1
        rec.append((spec, cur_first[0]))
        for q in range(j):
            if q not in held:
                released[q] = True
        if hold:
            held.add(j)
        if plan is not None:
            assert plan[j][0] == spec, (plan[j], spec)
        _try_issue()
        if plan is not None:
            assert issued[0] > j, "slab not issued (buffer still held)"
        return wsl[j % NB], b_wsl[j % NB], j

    def slab_release(j):
        held.discard(j)
        released[j] = True
        _try_issue()

    def rmsnorm(T, goff, presq=False):
        for c in range(0 if not presq else DC, DC):
            op("act", lambda e, c=c: e.activation(out=slot(c)[:, 0:T], in_=xT[:, c, 0:T], func=AF.Square),
               reads=[b_xT[c]], writes=[b_slot[c]])
        k = ps_next()

        def burst(e):
            for c in range(DC):
                i = e.matmul(psum[k][:, 0:T], lhsT=onesb[:], rhs=slot(c)[:, 0:T], start=(c == 0), stop=(c == DC - 1))
            return i

        op("pe", burst, reads=b_slot[0:DC] + [b_const], writes=[b_ps[k]])
        rt = TP.get()
        op("act", lambda e: e.activation(out=rt.a[:, 0:T], in_=psum[k][:, 0:T], func=AF.Sqrt, scale=1.0 / D, bias=EPS),
           reads=[b_ps[k]], writes=[rt.b])
        op("dve", lambda e: e.reciprocal(out=rt.a[:, 0:T], in_=rt.a[:, 0:T]), writes=[rt.b])
        for c in range(DC):
            op("dve", lambda e, c=c: e.scalar_tensor_tensor(out=slot(c)[:, 0:T], in0=xT[:, c, 0:T], scalar=pcol(goff, c),
                                                           in1=rt.a[:, 0:T], op0=ALU.mult, op1=ALU.mult),
               reads=[b_xT[c], rt.b, b_prm], writes=[b_slot[c]])

    def proj_A(kname, cgs, K_chunks, rhs_fn, T, evac, fine=True):
        cgs = list(cgs)
        for cg in cgs:
            banks = [ps_next() for _ in range(4)]
            kgs = [(k0, min(16, K_chunks - k0)) for k0 in range(0, K_chunks, 16)]
            ng = len(kgs)
            for gi, (k0, nk) in enumerate(kgs):
                wt, wbuf, _ = slab((kname, k0, nk, cg * 512, 512))
                rhs = [rhs_fn(k0 + kc) for kc in range(nk)]
                if fine and cg == cgs[0] and gi == 0:
                    for kc in range(nk):
                        for oc in range(4):
                            def one(e, oc=oc, wt=wt, nk=nk, kc=kc, bk=banks[oc], rhs=rhs, ng=ng):
                                return e.matmul(psum[bk][:, 0:T], lhsT=wt[:, kc, oc * 128:(oc + 1) * 128],
                                                rhs=rhs[kc][0][:, 0:T], start=(kc == 0), stop=(ng == 1 and kc == nk - 1))
                            if kc == 0:
                                op("pe", one, reads=[wbuf, rhs[kc][1]], writes=[b_ps[banks[oc]]])
                            else:
                                op("pe", one, reads=[wbuf, rhs[kc][1]], pwrites=[b_ps[banks[oc]]])
                    continue
                for oc in range(4):
                    def burst(e, oc=oc, wt=wt, nk=nk, gi=gi, bk=banks[oc], rhs=rhs, ng=ng):
                        for kc in range(nk):
                            i = e.matmul(psum[bk][:, 0:T], lhsT=wt[:, kc, oc * 128:(oc + 1) * 128],
                                         rhs=rhs[kc][0][:, 0:T],
                                         start=(gi == 0 and kc == 0), stop=(gi == ng - 1 and kc == nk - 1))
                        return i
                    rd = [wbuf] + [r[1] for r in rhs]
                    if gi == 0:
                        op("pe", burst, reads=rd, writes=[b_ps[banks[oc]]])
                    else:
                        op("pe", burst, reads=rd, pwrites=[b_ps[banks[oc]]])
            for oc in range(4):
                evac(cg * 4 + oc, banks[oc])

    def xn_rhs(kc):
        return slot(kc), b_slot[kc]

    def s16_rhs(kc):
        return slot(16 + kc), b_slot[16 + kc]

    def resid_add(T, sq=False):
        def ev(ocg, bk):
            op("dve", lambda e: e.tensor_tensor(out=xT[:, ocg, 0:T], in0=psum[bk][:, 0:T], in1=xT[:, ocg, 0:T], op=ALU.add),
               reads=[b_ps[bk]], writes=[b_xT[ocg]])
            if sq:
                op("act", lambda e: e.activation(out=slot(ocg)[:, 0:T], in_=xT[:, ocg, 0:T], func=AF.Square),
                   reads=[b_xT[ocg]], writes=[b_slot[ocg]])
        return ev

    def tile(kind, s, ti):
        prompt = kind == "p"
        T = TT if prompt else NSEQ * SS
        segs = [(0, TT, s)] if prompt else [(b * SS, SS, b) for b in range(NSEQ)]

        def pad0(si, hw):
            return sum(l + hw for (_, l, _) in segs[:si])

        if prompt:
            def qt(m):
                return slot(16 + m), b_slot[16 + m]

            def on(m):
                return slot(32 + m), b_slot[32 + m]
        else:
            def qt(m):
                off = 16 * 512 + m * 64
                return arena[:, off:off + 64], b_slot[off // 512]

            def on(m):
                off = 18 * 512 + m * 64
                return arena[:, off:off + 64], b_slot[off // 512]

        def std_bursts(wt, wbuf, banks, fine=False):
            if fine:
                for kc in range(16):
                    for oc in range(4):
                        def one(e, oc=oc, kc=kc, bk=banks[oc]):
                            return e.matmul(psum[bk][:, 0:T], lhsT=wt[:, kc, oc * 128:(oc + 1) * 128], rhs=slot(kc)[:, 0:T],
                                            start=(kc == 0), stop=(kc == 15))
                        if kc == 0:
                            op("pe", one, reads=[wbuf, b_slot[kc]], writes=[b_ps[banks[oc]]])
                        else:
                            op("pe", one, reads=[wbuf, b_slot[kc]], pwrites=[b_ps[banks[oc]]])
                return
            for oc in range(4):
                def burst(e, oc=oc, bk=banks[oc]):
                    for kc in range(16):
                        i = e.matmul(psum[bk][:, 0:T], lhsT=wt[:, kc, oc * 128:(oc + 1) * 128], rhs=slot(kc)[:, 0:T],
                                     start=(kc == 0), stop=(kc == 15))
                    return i
                op("pe", burst, reads=[wbuf] + b_slot[0:16], writes=[b_ps[banks[oc]]])

        def load_x(cg):
            xin = []
            if prompt:
                for tb in range(4):
                    t = TP.get()
                    src = xp[s, ti * TT + tb * 128:ti * TT + (tb + 1) * 128, cg * 512:(cg + 1) * 512]
                    fw.dma("sp", lambda e, t=t, src=src: e.dma_start(out=t.a[:, 0:512], in_=src), t.s, writes=[t.b])
                    xin.append((t, 128, tb * 128))
            else:
                t = TP.get()

                def go(e, t=t):
                    return [e.dma_start(out=t.a[b * SS:(b + 1) * SS, 0:512], in_=xs[b, :, cg * 512:(cg + 1) * 512]) for b in range(NSEQ)]
                fw.dma("sp", go, t.s, writes=[t.b], n=NSEQ)
                xin.append((t, NSEQ * SS, 0))
            for cl in range(4):
                c = cg * 4 + cl
                k = ps_next()

                def burst(e, k=k, cl=cl):
                    for (t, ntok, col0) in xin:
                        i = e.transpose(out=psum[k][:, col0:col0 + ntok], in_=t.a[0:ntok, cl * 128:(cl + 1) * 128],
                                        identity=ident[0:ntok, 0:ntok])
                    return i
                op("pe", burst, reads=[t.b for (t, _, _) in xin] + [b_const], writes=[b_ps[k]])
                op("act", lambda e, k=k, c=c: e.activation(out=xT[:, c, 0:T], in_=psum[k][:, 0:T], func=AF.Identity),
                   reads=[b_ps[k]], writes=[b_xT[c]])

        for cg in range(4):
            load_x(cg)

        rmsnorm(T, P_RGN)
        HY0 = 16
        chain = {}
        gate_v, gate_b = av(32 * 512, (16, 256))
        if cur_first[0]:
            fw.dma("pool", lambda e: e.dma_start(out=gate_v, in_=W32["gate"].rearrange("(n c) d -> c n d", c=128)),
                   s_misc, writes=gate_b)
            fw.dma("sp", lambda e: e.dma_start(out=WB["gate"].rearrange("(n c) d -> c n d", c=128), in_=gate_v),
                   s_gwb, reads=gate_b, pwrites=[b_wb["gate"]])
        else:
            fw.dma("sp", lambda e: e.dma_start(out=gate_v, in_=WB["gate"].rearrange("(n c) d -> c n d", c=128)),
                   s_misc, reads=[b_wb["gate"]], writes=gate_b)

        def rg_rec1(n, bk):
            rb = TP_RB.get()
            xc = TP_XC.get()
            cw = lambda kk: pcol(P_RGCW, n * 4 + kk)
            for si, (c0, ln, b) in enumerate(segs):
                p0 = pad0(si, 3)
                op(EP(), lambda e, p0=p0, b=b: e.tensor_copy(out=rb.a[:, p0:p0 + 3], in_=rgc_v[:, n, b, :]),
                   reads=[b_rgc[n]], writes=[rb.b])
                op("act", lambda e, p0=p0, c0=c0, ln=ln: e.activation(out=rb.a[:, p0 + 3:p0 + 3 + ln],
                                                                      in_=psum[bk][:, c0:c0 + ln], func=AF.Identity),
                   reads=[b_ps[bk]], writes=[rb.b])
            for si, (c0, ln, b) in enumerate(segs):
                p0 = pad0(si, 3)
                op(EP(), lambda e, p0=p0, b=b, ln=ln: e.tensor_copy(out=rgc_v[:, n, b, :], in_=rb.a[:, p0 + ln:p0 + ln + 3]),
                   reads=[rb.b], writes=[b_rgc[n]])
                op("dve", lambda e, p0=p0, c0=c0, ln=ln: e.tensor_scalar(
                    out=xc.a[:, c0:c0 + ln], in0=rb.a[:, p0:p0 + ln], scalar1=cw(0), scalar2=pcol(P_RGCB, n),
                    op0=ALU.mult, op1=ALU.add), reads=[rb.b, b_prm], writes=[xc.b])
                for kk in (1, 2, 3):
                    op("dve", lambda e, p0=p0, c0=c0, ln=ln, kk=kk: e.scalar_tensor_tensor(
                        out=xc.a[:, c0:c0 + ln], in0=rb.a[:, p0 + kk:p0 + kk + ln], scalar=cw(kk), in1=xc.a[:, c0:c0 + ln],
                        op0=ALU.mult, op1=ALU.add), reads=[rb.b, b_prm], writes=[xc.b])
            chain[n] = [xc, None]

        def rg_rec2(n):
            xc = chain[n][0]
            xb = TP16.get()
            op(EP(), lambda e: e.tensor_copy(out=xb.a[:, 0:T], in_=xc.a[:, 0:T]), reads=[xc.b], writes=[xb.b])
            chain[n][1] = xb

        RBK = [0, 1, 2, 3]
        GBK = [4, 5]
        KBK = [6, 7]
        slabs_g = {}
        slabs_r = {}

        def one_burst(wt, wbuf, oc, bk):
            def burst(e):
                for kc in range(16):
                    i = e.matmul(psum[bk][:, 0:T], lhsT=wt[:, kc, oc * 128:(oc + 1) * 128], rhs=slot(kc)[:, 0:T],
                                 start=(kc == 0), stop=(kc == 15))
                return i
            op("pe", burst, reads=[wbuf] + b_slot[0:16], writes=[b_ps[bk]])

        def g_burst(n):
            kg, oc = n // 4, n % 4
            if oc == 0:
                slabs_g[kg] = slab(("rg_in", 0, 16, kg * 512, 512), hold=True)
            wt, wbuf, sj = slabs_g[kg]
            bk = GBK[n % 2]
            one_burst(wt, wbuf, oc, bk)
            if oc == 3:
                slab_release(sj)
            op("act", lambda e: e.activation(out=slot(HY0 + n)[:, 0:T], in_=psum[bk][:, 0:T], func=AF.Gelu_apprx_tanh),
               reads=[b_ps[bk]], writes=[b_slot[HY0 + n]])

        def r_burst(n):
            kg, oc = n // 4, n % 4
            if oc == 0:
                slabs_r[kg] = slab(("rg_in", 0, 16, D + kg * 512, 512), hold=True)
            wt, wbuf, sj = slabs_r[kg]
            one_burst(wt, wbuf, oc, RBK[n % 4])
            if oc == 3:
                slab_release(sj)

        backs = {}

        def rg_backA(n):
            xc, xb = chain.pop(n)
            kr = KBK[0]
            ki = KBK[1]
            op("pe", lambda e: e.matmul(psum[kr][:, 0:T], lhsT=gate_v[:, n, 0:128], rhs=xb.a[:, 0:T], start=True, stop=True),
               reads=list(gate_b) + [xb.b], writes=[b_ps[kr]])
            op("pe", lambda e: e.matmul(psum[ki][:, 0:T], lhsT=gate_v[:, n, 128:256], rhs=xb.a[:, 0:T], start=True, stop=True),
               reads=list(gate_b) + [xb.b], writes=[b_ps[ki]])
            r = TP_G.get()
            it = TP_G.get()
            a = TP_G.get()
            m = TP_G.get()
            op("act", lambda e: e.activation(out=r.a[:, 0:T], in_=psum[kr][:, 0:T], func=AF.Tanh, scale=0.5, bias=HGB(n * 2)),
               reads=[b_ps[kr], b_dp], writes=[r.b])
            op("act", lambda e: e.activation(out=it.a[:, 0:T], in_=psum[ki][:, 0:T], func=AF.Tanh, scale=0.5, bias=HGB(n * 2 + 1)),
               reads=[b_ps[ki], b_dp], writes=[it.b])
            op("act", lambda e: e.activation(out=a.a[:, 0:T], in_=r.a[:, 0:T], func=AF.Exp, scale=HCL(n), bias=HCL(n)),
               reads=[r.b, b_dp], writes=[a.b])
            op("act", lambda e: e.activation(out=m.a[:, 0:T], in_=r.a[:, 0:T], func=AF.Exp, scale=CL(n), bias=CL(n)),
               reads=[r.b, b_dp], writes=[m.b])
            op("act", lambda e: e.activation(out=m.a[:, 0:T], in_=m.a[:, 0:T], func=AF.Sqrt, scale=-0.25, bias=0.25),
               writes=[m.b])
            op("dve", lambda e: e.scalar_tensor_tensor(out=it.a[:, 0:T], in0=it.a[:, 0:T], scalar=1.0, in1=m.a[:, 0:T],
                                                       op0=ALU.add, op1=ALU.mult), reads=[m.b], writes=[it.b])
            op(EP(), lambda e: e.tensor_tensor(out=m.a[:, 0:T], in0=it.a[:, 0:T], in1=xc.a[:, 0:T], op=ALU.mult),
               reads=[it.b, xc.b], writes=[m.b])
            backs[n] = (r, a, m)

        def rg_backB(n):
            h, a, m = backs.pop(n)
            for si, (c0, ln, b) in enumerate(segs):
                op("dve", lambda e, c0=c0, ln=ln, b=b: e.tensor_tensor_scan(
                    out=h.a[:, c0:c0 + ln], data0=a.a[:, c0:c0 + ln], data1=m.a[:, c0:c0 + ln], initial=rgh_v[:, n, b:b + 1],
                    op0=ALU.mult, op1=ALU.add), reads=[a.b, m.b, b_rgh[n]], writes=[h.b])
                op(EP(), lambda e, c0=c0, ln=ln, b=b: e.tensor_copy(out=rgh_v[:, n, b:b + 1], in_=h.a[:, c0 + ln - 1:c0 + ln]),
                   reads=[h.b], writes=[b_rgh[n]])
            op(EP(), lambda e: e.tensor_tensor(out=slot(HY0 + n)[:, 0:T], in0=h.a[:, 0:T], in1=slot(HY0 + n)[:, 0:T], op=ALU.mult),
               reads=[h.b], writes=[b_slot[HY0 + n]])

        r_burst(0)
        for i_ in range(16 + 6):
            if 0 <= i_ - 5 < 16:
                rg_backB(i_ - 5)
            if 0 <= i_ - 4 < 16:
                rg_backA(i_ - 4)
            if i_ < 16:
                g_burst(i_)
            if i_ + 1 < 16:
                r_burst(i_ + 1)
            if i_ < 16:
                rg_rec1(i_, RBK[i_ % 4])
                rg_rec2(i_)
        proj_A("rg_out", range(4), 16, s16_rhs, T, resid_add(T, sq=True))

        def ffn(l):
            rmsnorm(T, P_FFN + l * 16, presq=True)
            H0 = 16
            up = "up%d" % l
            pend = {}

            prev = [None]

            def up_s12(j, bk):
                ub = TP.get()
                cc = TP.get()
                fcw = lambda kk: pcol(P_FCW, (l * 88 + j) * 3 + kk)
                for si, (c0, ln, b) in enumerate(segs):
                    p0 = pad0(si, 2)
                    op(EP(), lambda e, p0=p0, b=b: e.tensor_copy(out=ub.a[:, p0:p0 + 2], in_=ffn_v[:, l, j, b, :]),
                       reads=[b_ffn[l][j]], writes=[ub.b])
                    op("act", lambda e, p0=p0, c0=c0, ln=ln: e.activation(
                        out=ub.a[:, p0 + 2:p0 + 2 + ln], in_=psum[bk][:, c0:c0 + ln], func=AF.Identity),
                       reads=[b_ps[bk]], writes=[ub.b])
                op("act", lambda e: e.activation(out=cc.a[:, 0:T], in_=psum[bk][:, 0:T], func=AF.Identity,
                                                 scale=fcw(2), bias=pcol(P_FCB, l * 88 + j)),
                   reads=[b_ps[bk], b_prm], writes=[cc.b])
                for si, (c0, ln, b) in enumerate(segs):
                    p0 = pad0(si, 2)
                    op(EP(), lambda e, p0=p0, b=b, ln=ln: e.tensor_copy(out=ffn_v[:, l, j, b, :], in_=ub.a[:, p0 + ln:p0 + ln + 2]),
                       reads=[ub.b], writes=[b_ffn[l][j]])
                    for kk in (1, 0):
                        op("dve", lambda e, p0=p0, c0=c0, ln=ln, kk=kk: e.scalar_tensor_tensor(
                            out=cc.a[:, c0:c0 + ln], in0=ub.a[:, p0 + kk:p0 + kk + ln], scalar=fcw(kk), in1=cc.a[:, c0:c0 + ln],
                            op0=ALU.mult, op1=ALU.add), reads=[ub.b, b_prm], writes=[cc.b])
                return cc

            def up_s3(j, cc):
                if j < FC:
                    op("act", lambda e: e.activation(out=cc.a[:, 0:T], in_=cc.a[:, 0:T], func=AF.Gelu_apprx_tanh), writes=[cc.b])
                    pend[j] = cc
                else:
                    g = pend.pop(j - FC)
                    op(EP(), lambda e: e.tensor_tensor(out=slot(H0 + j - FC)[:, 0:T], in0=g.a[:, 0:T], in1=cc.a[:, 0:T], op=ALU.mult),
                       reads=[g.b, cc.b], writes=[b_slot[H0 + j - FC]])

            def up_slab(cgi):
                banks = [ps_next() for _ in range(4)]
                wt, wbuf, _ = slab((up, 0, 16, cgi * 512, 512))
                std_bursts(wt, wbuf, banks, fine=(cgi == 0))
                for oc in range(4):
                    j = cgi * 4 + oc
                    cc = up_s12(j, banks[oc])
                    if prev[0] is not None:
                        up_s3(*prev[0])
                    prev[0] = (j, cc)

            for kk in range(11):
                up_slab(kk)
                up_slab(11 + kk)
            up_s3(*prev[0])
            proj_A("down%d" % l, range(4), FC, s16_rhs, T, resid_add(T, sq=True))

        ffn(0)

        rmsnorm(T, P_ATN, presq=True)
        ktn_v, ktn_b = (None, None)
        vn_v, vn_b = (None, None)
        if not prompt:
            ktn_v, ktn_b = av(20 * 512, (16, 64))
            vn_v, vn_b = av(22 * 512, (2, 2048))

        tokblocks = [(tb * 128, 128) for tb in range(4)] if prompt else [(b * SS, SS) for b in range(NSEQ)]

        def out_rows(dst_p, dst_s, bi):
            if prompt:
                return dst_p[s, ti * TT + bi * 128:ti * TT + (bi + 1) * 128, :]
            return dst_s[bi]

        def bform(wt, wbuf, cgx, dst_p, dst_s, extra):
            for bi, (c0, ntok) in enumerate(tokblocks):
                k = ps_next()

                def burst(e, k=k, c0=c0, ntok=ntok):
                    for kc in range(16):
                        i = e.matmul(psum[k][0:ntok, :], lhsT=slot(kc)[:, c0:c0 + ntok], rhs=wt[:, kc, :], start=(kc == 0), stop=(kc == 15))
                    return i
                op("pe", burst, reads=[wbuf] + b_slot[0:16], writes=[b_ps[k]])
                ot = TP.get()
                op("act", lambda e, k=k, ntok=ntok, ot=ot: e.activation(out=ot.a[0:ntok, 0:512], in_=psum[k][0:ntok, :], func=AF.Identity),
                   reads=[b_ps[k]], writes=[ot.b])
                dst = out_rows(dst_p, dst_s, bi)[:, cgx * 512:(cgx + 1) * 512]
                fw.dma("sp", lambda e, ot=ot, ntok=ntok, dst=dst: e.dma_start(out=dst, in_=ot.a[0:ntok, 0:512]),
                       ot.s, reads=[ot.b])
                if extra is not None:
                    extra(ot, bi, ntok)

        def k_evac(m, bk):
            kf = TP.get()
            op("act", lambda e: e.activation(out=kf.a[:, 0:T], in_=psum[bk][:, 0:T], func=AF.Identity),
               reads=[b_ps[bk]], writes=[kf.b])
            if prompt:
                kt = TP16.get()
                op(EP(), lambda e: e.tensor_copy(out=kt.a[:, 0:T], in_=kf.a[:, 0:T]), reads=[kf.b], writes=[kt.b])
                fw.dma("sp", lambda e: e.dma_start(out=ktT[s, m, :, ti * TT:(ti + 1) * TT], in_=kt.a[:, 0:T]),
                       kt.s, reads=[kt.b], pwrites=[b_kt[s]])
            else:
                op(EP(), lambda e: e.tensor_copy(out=ktn_v[:, m, :], in_=kf.a[:, 0:T]), reads=[kf.b], pwrites=ktn_b)
            return kf

        def k_out(cgk, kfs):
            for bi, (c0, ntok) in enumerate(tokblocks):
                k = ps_next()

                def burst(e, k=k, c0=c0, ntok=ntok):
                    for oc in range(4):
                        i = e.transpose(out=psum[k][0:ntok, oc * 128:(oc + 1) * 128], in_=kfs[oc].a[:, c0:c0 + ntok], identity=ident[:])
                    return i
                op("pe", burst, reads=[x.b for x in kfs] + [b_const], writes=[b_ps[k]])
                ot = TP.get()
                op("act", lambda e, k=k, ntok=ntok, ot=ot: e.activation(out=ot.a[0:ntok, 0:512], in_=psum[k][0:ntok, :], func=AF.Identity),
                   reads=[b_ps[k]], writes=[ot.b])
                dst = out_rows(k_p, k_s, bi)[:, cgk * 512:(cgk + 1) * 512]
                fw.dma("sp", lambda e, ot=ot, ntok=ntok, dst=dst: e.dma_start(out=dst, in_=ot.a[0:ntok, 0:512]),
                       ot.s, reads=[ot.b])

        for cgk in range(4):
            wt, wbuf, _ = slab(("qkv", 0, 16, D + cgk * 512, 512))
            banks = [ps_next() for _ in range(4)]
            std_bursts(wt, wbuf, banks, fine=(cgk == 0))
            kfs = [k_evac(cgk * 4 + oc, banks[oc]) for oc in range(4)]
            k_out(cgk, kfs)

        def v_extra_fn(cgv):
            def v_extra(ot, bi, ntok):
                if prompt:
                    vt = TP16.get()
                    op(EP(), lambda e: e.tensor_copy(out=vt.a[:, :], in_=ot.a[:, 0:512]), reads=[ot.b], writes=[vt.b])
                    r0 = ti * TT + bi * 128
                    fw.dma("sp", lambda e: e.dma_start(out=vsc[s, r0:r0 + 128, cgv * 512:(cgv + 1) * 512], in_=vt.a[:, :]),
                           vt.s, reads=[vt.b], pwrites=[b_vs[s]])
                else:
                    op(EP(), lambda e: e.tensor_copy(out=vn_v[0:ntok, bi, cgv * 512:(cgv + 1) * 512], in_=ot.a[0:ntok, 0:512]),
                       reads=[ot.b], pwrites=vn_b)
            return v_extra

        for cgv in range(4):
            wt, wbuf, _ = slab(("qkv", 0, 16, 2 * D + cgv * 512, 512))
            bform(wt, wbuf, cgv, v_p, v_s, v_extra_fn(cgv))

        def evq(ocg, bk):
            qa, qb = qt(ocg)
            op("act", lambda e: e.activation(out=qa[:, 0:T], in_=psum[bk][:, 0:T], func=AF.Identity),
               reads=[b_ps[bk]], pwrites=[qb])
        proj_A("qkv", range(4), 16, xn_rhs, T, evq)

        OB = [[0, 1, 2], [0, 1, 2]]
        SB = [3, 4, 5, 6, 7]
        LOOK = 4

        pend_tail = [None]

        def attn_unit(h, qc0, qn, kblocks):
            on_t = {}
            nkb = len(kblocks)

            def one_map(j):
                pts = {}
                qa, qb = qt(2 * h + j)
                ob = OB[j]

                def emit_S(kb):
                    ktf, vap, vbufs, nk, qoff, diag = kblocks[kb]
                    ktap, ktbufs = ktf(j)
                    sbk = SB[si_[0] % len(SB)]
                    si_[0] += 1
                    op("pe", lambda e: e.matmul(psum[sbk][0:nk, qoff:qn], lhsT=ktap, rhs=qa[:, qc0 + qoff:qc0 + qn], start=True, stop=True),
                       reads=list(ktbufs) + [qb], writes=[b_ps[sbk]])
                    pt = TP16.get()
                    op("act", lambda e: e.activation(out=pt.a[0:nk, qoff:qn], in_=psum[sbk][0:nk, qoff:qn], func=AF.Exp, scale=QSCALE),
                       reads=[b_ps[sbk]], writes=[pt.b])
                    if diag:
                        op(EP(), lambda e: e.memset(pt.a[64:128, qoff:qoff + 64], 0.0), writes=[pt.b])
                    pts[kb] = pt

                def emit_PV(kb):
                    ktf, vap, vbufs, nk, qoff, diag = kblocks[kb]
                    pt = pts.pop(kb)
                    first = kb == 0
                    last = kb == nkb - 1

                    def burst(e):
                        e.matmul(psum[ob[0]][:, qoff:qn], lhsT=vap[0:nk, 0:128], rhs=pt.a[0:nk, qoff:qn], start=first, stop=last)
                        e.matmul(psum[ob[1]][:, qoff:qn], lhsT=vap[0:nk, 128:256], rhs=pt.a[0:nk, qoff:qn], start=first, stop=last)
                        return e.matmul(psum[ob[2]][:, qoff:qn], lhsT=onesb[0:nk, :], rhs=pt.a[0:nk, qoff:qn], start=first, stop=last)
                    if first:
                        op("pe", burst, reads=list(vbufs) + [pt.b, b_const], writes=[b_ps[x] for x in ob])
                    else:
                        op("pe", burst, reads=list(vbufs) + [pt.b, b_const], pwrites=[b_ps[x] for x in ob])

                for kb in range(min(LOOK, nkb)):
                    emit_S(kb)
                for kb in range(nkb):
                    if kb + LOOK < nkb:
                        emit_S(kb + LOOK)
                    emit_PV(kb)
                ri = TP.get()
                op("act", lambda e: e.activation(out=ri.a[:, 0:qn], in_=psum[ob[2]][:, 0:qn], func=AF.Ln), reads=[b_ps[ob[2]]], writes=[ri.b])
                tjs = []
                for e_ in range(2):
                    tj = TP.get()
                    op("dve", lambda e, tj=tj, e_=e_: e.tensor_copy(out=tj.a[:, 0:qn], in_=psum[ob[e_]][:, 0:qn]),
                       reads=[b_ps[ob[e_]]], writes=[tj.b])
                    tjs.append(tj)
                op("act", lambda e: e.activation(out=ri.a[:, 0:qn], in_=ri.a[:, 0:qn], func=AF.Exp, scale=-1.0), writes=[ri.b])
                for e_ in range(2):
                    tj = tjs[e_]
                    op("dve", lambda e, tj=tj: e.tensor_tensor(out=tj.a[:, 0:qn], in0=tj.a[:, 0:qn], in1=ri.a[:, 0:qn], op=ALU.mult),
                       reads=[ri.b], writes=[tj.b])
                    on_t[(j, e_)] = tj

            one_map(0)
            if pend_tail[0] is not None:
                pend_tail[0]()
                pend_tail[0] = None
            one_map(1)
            pend_tail[0] = lambda: attn_tail(h, qc0, qn, on_t)

        def attn_tail(h, qc0, qn, on_t):
            sq = []
            for e_ in range(2):
                t0 = on_t[(0, e_)]
                t1 = on_t[(1, e_)]
                op("dve", lambda e, t0=t0, t1=t1: e.scalar_tensor_tensor(out=t0.a[:, 0:qn], in0=t1.a[:, 0:qn], scalar=NEGLAM, in1=t0.a[:, 0:qn],
                                                                        op0=ALU.mult, op1=ALU.add), reads=[t1.b, b_dp], writes=[t0.b])
                q_ = TP16.get()
                op("act", lambda e, t0=t0, q_=q_: e.activation(out=q_.a[:, 0:qn], in_=t0.a[:, 0:qn], func=AF.Square), reads=[t0.b], writes=[q_.b])
                sq.append(q_)
            sbk = SB[si_[0] % len(SB)]
            si_[0] += 1

            def burst(e):
                e.matmul(psum[sbk][:, 0:qn], lhsT=onesb[:], rhs=sq[0].a[:, 0:qn], start=True, stop=False)
                return e.matmul(psum[sbk][:, 0:qn], lhsT=onesb[:], rhs=sq[1].a[:, 0:qn], start=False, stop=True)
            op("pe", burst, reads=[sq[0].b, sq[1].b, b_const], writes=[b_ps[sbk]])
            rs = TP.get()
            op("act", lambda e: e.activation(out=rs.a[:, 0:qn], in_=psum[sbk][:, 0:qn], func=AF.Ln, scale=1.0 / (2 * HD), bias=EPS),
               reads=[b_ps[sbk]], writes=[rs.b])
            op("act", lambda e: e.activation(out=rs.a[:, 0:qn], in_=rs.a[:, 0:qn], func=AF.Exp, scale=-0.5), writes=[rs.b])
            for e_ in range(2):
                t0 = on_t[(0, e_)]
                oa, obuf = on(2 * h + e_)
                op("dve", lambda e, t0=t0, e_=e_, oa=oa: e.scalar_tensor_tensor(out=oa[:, qc0:qc0 + qn], in0=t0.a[:, 0:qn], scalar=GS(e_),
                                                                             in1=rs.a[:, 0:qn], op0=ALU.mult, op1=ALU.mult),
                   reads=[t0.b, rs.b, b_dp], pwrites=[obuf])

        if prompt:
            nkb = 4 * (ti + 1)
            nkeys = 128 * nkb
            kvsets = [(0, 8), (48, 56)]

            def p_head(h):
                s0, s1 = kvsets[h % 2]
                ktv, ktb = av(s0 * 512, (2, 2048))
                vv, vb = av(s1 * 512, (16, 256))
                fw.dma("sp", lambda e: e.dma_start(out=ktv[:, :, 0:nkeys], in_=ktT[s, 2 * h:2 * h + 2, :, 0:nkeys].rearrange("m d t -> d m t")),
                       s_kv[(h % 2) * 2], reads=[b_kt[s]], writes=ktb)
                fw.dma("sp", lambda e: e.dma_start(out=vv[:, 0:nkb, :], in_=vsc[s, 0:nkeys, h * 256:(h + 1) * 256].rearrange("(kb p) e -> p kb e", p=128)),
                       s_kv[(h % 2) * 2 + 1], reads=[b_vs[s]], writes=vb)
                kblocks = []
                for kb in range(nkb):
                    jj = kb - 4 * ti
                    qoff = 0 if jj < 0 else 128 * jj
                    ktf = (lambda j, kb=kb: (ktv[:, j, kb * 128:(kb + 1) * 128], ktb))
                    kblocks.append((ktf, vv[:, kb, :], vb, 128, qoff, jj >= 0))
                attn_unit(h, 0, TT, kblocks)
            for h in range(NH):
                p_head(h)
        else:
            def s_head(b, h):
                par = (b * NH + h) % 2
                base = [(0, 8, 32), (40, 48, 56)][par]
                krv, krb = av(base[0] * 512, (16, 256))
                ktv, ktb = av(base[1] * 512, (2, 2048))
                vv, vb = av(base[2] * 512, (16, 256))
                fw.dma("pool", lambda e: e.dma_start(out=krv[:, 0:NPB, :], in_=ck[b, :, h * 256:(h + 1) * 256].rearrange("(kb p) e -> p kb e", p=128)),
                       s_kv[par * 2], writes=krb)
                fw.dma("pool", lambda e: e.dma_start(out=vv[:, 0:NPB, :], in_=cv[b, :, h * 256:(h + 1) * 256].rearrange("(kb p) e -> p kb e", p=128)),
                       s_kv[par * 2 + 1], writes=vb)
                gi_ = 0
                for j in range(2):
                    for g0 in range(0, NPB, 4):
                        k = ps_next()
                        nb_ = min(4, NPB - g0)

                        def burst(e, k=k, g0=g0, nb_=nb_, j=j):
                            for q in range(nb_):
                                i = e.matmul(psum[k][:, q * 128:(q + 1) * 128], lhsT=krv[:, g0 + q, j * 128:(j + 1) * 128], rhs=identb[:], start=True, stop=True)
                            return i
                        op("pe", burst, reads=list(krb) + [b_const], writes=[b_ps[k]])
                        if gi_ % 2 == 0:
                            op("act", lambda e, k=k, g0=g0, nb_=nb_, j=j: e.activation(out=ktv[:, j, g0 * 128:(g0 + nb_) * 128], in_=psum[k][:, 0:nb_ * 128], func=AF.Identity),
                               reads=[b_ps[k]], pwrites=ktb)
                        else:
                            op("dve", lambda e, k=k, g0=g0, nb_=nb_, j=j: e.tensor_copy(out=ktv[:, j, g0 * 128:(g0 + nb_) * 128], in_=psum[k][:, 0:nb_ * 128]),
                               reads=[b_ps[k]], pwrites=ktb)
                        gi_ += 1
                kblocks = []
                for kb in range(NPB):
                    ktf = (lambda j, kb=kb: (ktv[:, j, kb * 128:(kb + 1) * 128], ktb))
                    kblocks.append((ktf, vv[:, kb, :], vb, 128, 0, False))
                ktf = (lambda j: (ktn_v[:, 2 * h + j, b * SS:(b + 1) * SS], ktn_b))
                kblocks.append((ktf, vn_v[0:SS, b, h * 256:(h + 1) * 256], vn_b, SS, 0, False))
                attn_unit(h, b * SS, SS, kblocks)
            for b in range(NSEQ):
                for h in range(NH):
                    s_head(b, h)

        if pend_tail[0] is not None:
            pend_tail[0]()
            pend_tail[0] = None
        proj_A("at_out", range(4), 16, on, T, resid_add(T, sq=True))

        ffn(1)

        kf = ps_next()

        def burstf(e):
            for c in range(DC):
                i = e.matmul(psum[kf][:, 0:T], lhsT=onesb[:], rhs=slot(c)[:, 0:T], start=(c == 0), stop=(c == DC - 1))
            return i
        op("pe", burstf, reads=b_slot[0:DC] + [b_const], writes=[b_ps[kf]])
        op("act", lambda e: e.activation(out=rstd_sb[:, 0:T], in_=psum[kf][:, 0:T], func=AF.Sqrt, scale=1.0 / D, bias=EPS),
           reads=[b_ps[kf]], writes=[b_rstd])
        op("dve", lambda e: e.reciprocal(out=rstd_sb[:, 0:T], in_=rstd_sb[:, 0:T]), writes=[b_rstd])

        def final_cg(cg):
            yts = []
            for cl in range(4):
                c = cg * 4 + cl
                yt = TP.get()
                op("dve", lambda e, c=c, yt=yt: e.scalar_tensor_tensor(out=yt.a[:, 0:T], in0=xT[:, c, 0:T], scalar=pcol(P_FIN, c), in1=rstd_sb[:, 0:T],
                                                                     op0=ALU.mult, op1=ALU.mult), reads=[b_xT[c], b_rstd, b_prm], writes=[yt.b])
                yts.append(yt)
            for bi, (c0, ntok) in enumerate(tokblocks):
                k = ps_next()

                def burst(e, k=k, c0=c0, ntok=ntok):
                    for cl in range(4):
                        i = e.transpose(out=psum[k][0:ntok, cl * 128:(cl + 1) * 128], in_=yts[cl].a[:, c0:c0 + ntok], identity=ident[:])
                    return i
                op("pe", burst, reads=[y.b for y in yts] + [b_const], writes=[b_ps[k]])
                ot = TP.get()
                op("act", lambda e, k=k, ntok=ntok, ot=ot: e.activation(out=ot.a[0:ntok, 0:512], in_=psum[k][0:ntok, :], func=AF.Identity),
                   reads=[b_ps[k]], writes=[ot.b])
                dst = out_rows(y_p, y_s, bi)[:, cg * 512:(cg + 1) * 512]
                fw.dma("sp", lambda e, ot=ot, ntok=ntok, dst=dst: e.dma_start(out=dst, in_=ot.a[0:ntok, 0:512]),
                       ot.s, reads=[ot.b])
        for cg in range(4):
            final_cg(cg)


    op("dve", lambda e: e.memset(stb[:], 0.0), writes=all_st)
    cnt = 0
    for s in range(NSEQ):
        for ti in range(NTILE):
            if n_ptiles is not None and cnt >= n_ptiles:
                continue
            tile("p", s, ti)
            cnt += 1
            pool_ok[0] = True
            cur_first[0] = False
    fw.dma("sp", lambda e: e.dma_start(out=st_p[:], in_=stb[:]), s_st, reads=all_st)
    if do_sample:
        pool_ok[0] = True
        fw.dma("sp", lambda e: e.dma_start(out=stb[:], in_=st_in[:]), s_st, writes=all_st)
        tile("s", 0, 0)
        fw.dma("sp", lambda e: e.dma_start(out=st_s[:], in_=stb[:]), s_st, reads=all_st)
    e = fw.E["sp"]
    for name, val in fw.dsem_n.items():
        if val > 0:
            fw._wait(e, ("d", name, val))


_NC_CACHE = {}


def _fm(v):
    v = np.asarray(v)
    lead = v.shape[:-1]
    c = v.shape[-1] // 128
    v = v.reshape(lead + (c, 128))
    return np.moveaxis(v, -1, 0)


def _pack_prm(i):
    prm = np.zeros((128, NPRM), np.float32)
    prm[:, P_RGN:P_RGN + 16] = _fm(i["rg_norm"][0])
    prm[:, P_RGCW:P_RGCW + 64] = np.moveaxis(_fm(i["rg_conv_w"][0]), 1, 2).reshape(128, 64)
    prm[:, P_RGCB:P_RGCB + 16] = _fm(i["rg_conv_b"][0])
    gb = np.asarray(i["rg_gate_b"][0]).reshape(16, 2, 128)
    prm[:, P_RGGB:P_RGGB + 32] = np.transpose(gb, (2, 0, 1)).reshape(128, 32)
    prm[:, P_LL:P_LL + 16] = _fm(i["rg_log_lambda"][0])
    prm[:, P_ATN:P_ATN + 16] = _fm(i["at_norm"][0])
    prm[:, P_LAM:P_LAM + 4] = np.asarray(i["at_lambda"][0]).T
    prm[:, P_SUB:P_SUB + 2] = _fm(i["at_subln"][0])
    prm[:, P_FFN:P_FFN + 32] = _fm(i["ffn_norm"]).reshape(128, 32)
    fcw = _fm(i["ffn_conv_w"])
    prm[:, P_FCW:P_FCW + 528] = np.transpose(fcw, (0, 1, 3, 2)).reshape(128, 528)
    prm[:, P_FCB:P_FCB + 176] = _fm(i["ffn_conv_b"]).reshape(128, 176)
    prm[:, P_FIN:P_FIN + 16] = _fm(i["final_norm"])
    return prm


def _pack_state(rgc, rgh, ffn):
    st = np.zeros((128, NST), np.float32)
    a = _fm(rgc)
    st[:, S_RGC:S_RGC + 96] = np.transpose(a, (0, 3, 1, 2)).reshape(128, 96)
    a = _fm(rgh)
    st[:, S_RGH:S_RGH + 32] = np.transpose(a, (0, 2, 1)).reshape(128, 32)
    a = _fm(ffn)
    st[:, S_FFN:S_FFN + 704] = np.transpose(a, (0, 1, 4, 2, 3)).reshape(128, 704)
    return st


def _unpack_state(st):
    a = st[:, S_RGC:S_RGC + 96].reshape(128, 16, 2, 3)
    rgc = np.transpose(a, (2, 3, 1, 0)).reshape(2, 3, D)
    a = st[:, S_RGH:S_RGH + 32].reshape(128, 16, 2)
    rgh = np.transpose(a, (2, 1, 0)).reshape(2, D)
    a = st[:, S_FFN:S_FFN + 704].reshape(128, 2, 88, 2, 2)
    ffn = np.transpose(a, (1, 3, 4, 2, 0)).reshape(2, 2, 2, 2 * DFF)
    return rgc, rgh, ffn


def run(inputs, n_cores, SEQ, PAST, trace=False):
    key = (SEQ, PAST)
    if key not in _NC_CACHE:
        _NC_CACHE[key] = build_nc(SEQ, PAST)
    nc = _NC_CACHE[key]
    i = {k: np.asarray(v) for k, v in inputs.items()}
    prm = _pack_prm(i)
    ident = np.eye(128, dtype=np.float32)
    shared = {
        "prm": prm, "ident": ident,
        "w_rg_in": np.ascontiguousarray(i["rg_w_in"][0]),
        "w_gate": np.ascontiguousarray(i["rg_gate_w"][0].reshape(DC * 128, 256)),
        "w_rg_out": np.ascontiguousarray(i["rg_w_out"][0]),
        "w_up0": np.ascontiguousarray(i["ffn_w_up"][0]), "w_up1": np.ascontiguousarray(i["ffn_w_up"][1]),
        "w_down0": np.ascontiguousarray(i["ffn_w_down"][0]), "w_down1": np.ascontiguousarray(i["ffn_w_down"][1]),
        "w_qkv": np.ascontiguousarray(i["at_w_qkv"][0]),
        "w_at_out": np.ascontiguousarray(i["at_w_out"][0]),
    }
    in_maps = []
    for c in range(n_cores):
        b0 = c * NSEQ
        m = dict(shared)
        m["xp"] = np.ascontiguousarray(i["x_prompt"][b0:b0 + NSEQ])
        m["xs"] = np.ascontiguousarray(i["x_sample"][b0:b0 + NSEQ])
        m["st_in"] = _pack_state(i["state_rglru_conv"][0, b0:b0 + NSEQ], i["state_rglru_h"][0, b0:b0 + NSEQ],
                                 i["state_ffn_conv"][:, b0:b0 + NSEQ])
        m["ck"] = np.ascontiguousarray(i["cache_attn_k"][0, b0:b0 + NSEQ].reshape(NSEQ, PAST, D))
        m["cv"] = np.ascontiguousarray(i["cache_attn_v"][0, b0:b0 + NSEQ].reshape(NSEQ, PAST, D))
        in_maps.append(m)
    res = run_bass_kernel_spmd(nc, in_maps, core_ids=list(range(n_cores)), trace=trace)
    R = res.results
    cat = lambda k: np.concatenate([r[k] for r in R], axis=0)
    y_p = cat("y_p")
    y_s = cat("y_s")
    k_p = cat("k_p").reshape(1, n_cores * NSEQ, SEQ, 2 * NH, HD)
    v_p = cat("v_p").reshape(1, n_cores * NSEQ, SEQ, NH, 2 * HD)
    k_s = cat("k_s").reshape(1, n_cores * NSEQ, SS, 2 * NH, HD)
    v_s = cat("v_s").reshape(1, n_cores * NSEQ, SS, NH, 2 * HD)
    sp = [_unpack_state(r["st_p"]) for r in R]
    ss = [_unpack_state(r["st_s"]) for r in R]
    rgc_p = np.concatenate([x[0] for x in sp], 0)[None]
    rgh_p = np.concatenate([x[1] for x in sp], 0)[None]
    ffn_p = np.concatenate([x[2] for x in sp], 1)
    rgc_s = np.concatenate([x[0] for x in ss], 0)[None]
    rgh_s = np.concatenate([x[1] for x in ss], 0)[None]
    ffn_s = np.concatenate([x[2] for x in ss], 1)
    outs = (y_p, y_s, rgc_p, rgh_p, k_p, v_p, ffn_p, rgc_s, rgh_s, k_s, v_s, ffn_s)
    outs = tuple(np.ascontiguousarray(o, dtype=np.float32) for o in outs)
    return outs, res


def kernel(**inputs):
    outs, _ = run(inputs, 8, 2048, 2048)
    return outs
```

```python
import math
import numpy as np
from contextlib import ExitStack
import concourse.bass as bass
import concourse.mybir as mybir
from concourse.bass_utils import run_bass_kernel_spmd

F32 = mybir.dt.float32
BF16 = mybir.dt.bfloat16
AF = mybir.ActivationFunctionType
ALU = mybir.AluOpType

D = 2048
DC = 16
DFF = 5632
FC = 44
NH = 8
HD = 128
TT = 512
EPS = 1e-6
NSEQ = 2
SS = 32
RG_C = 8.0
LAM_INIT = 0.8 - 0.6 * math.exp(-0.3 * 1)
QSCALE = HD ** -0.5
NB = 3
NTMP = 16
NT16 = 8
TW = 520

P_RGN = 0
P_RGCW = 16
P_RGCB = 80
P_RGGB = 96
P_LL = 128
P_ATN = 144
P_LAM = 160
P_SUB = 164
P_FFN = 166
P_FCW = 198
P_FCB = 726
P_FIN = 902
NPRM = 918
S_RGC = 0
S_RGH = 96
S_FFN = 128
NST = 832


class Eng:
    def __init__(self, key, sem):
        self.key = key
        self.sem = sem
        self.n = 0
        self.prog = []
        self.seen = {}


class Buf:
    __slots__ = ("name", "w", "r")

    def __init__(self, name):
        self.name = name
        self.w = {}
        self.r = {}


class FW:
    def __init__(self, nc, stack):
        self.nc = nc
        self.stack = stack
        self.sems = {}
        self.E = {}
        for k in ("pe", "act", "dve", "pool", "sp"):
            self.E[k] = Eng(k, self._sem("eng_" + k))
        self.dsem_n = {}

    def _sem(self, name):
        if name not in self.sems:
            self.sems[name] = self.stack.enter_context(self.nc.semaphore(name))
        return self.sems[name]

    def reset(self):
        for e in self.E.values():
            e.n = 0
            e.prog = []
            e.seen = {}
        self.dsem_n = {}

    def dsem(self, name):
        s = self._sem(name)
        self.dsem_n.setdefault(name, 0)
        return name

    def _wait(self, e, tok):
        kind, obj, val = tok
        if kind == "e":
            if obj is e and val <= e.n - 3:
                return
            if e.seen.get(obj.key, 0) >= val:
                return
            e.seen[obj.key] = val
            sem = obj.sem
        else:
            if e.seen.get(obj, 0) >= val:
                return
            e.seen[obj] = val
            sem = self.sems[obj]
        e.prog.append(lambda eng, sem=sem, val=val: eng.wait_ge(sem, val))

    def _deps(self, e, reads, writes, pwrites):
        for b in reads:
            for t in b.w.values():
                self._wait(e, t)
        for b in writes:
            for t in b.w.values():
                self._wait(e, t)
            for t in b.r.values():
                self._wait(e, t)
        for b in pwrites:
            for t in b.r.values():
                self._wait(e, t)

    def _mark(self, tok, key, reads, writes, pwrites):
        for b in reads:
            b.r[key] = tok
        for b in writes:
            b.w = {key: tok}
            b.r = {}
        for b in pwrites:
            b.w[key] = tok

    def op(self, ek, fn, reads=(), writes=(), pwrites=()):
        e = self.E[ek]
        self._deps(e, reads, writes, pwrites)
        e.n += 1
        tok = ("e", e, e.n)
        e.prog.append(lambda eng, fn=fn, sem=e.sem: fn(eng).then_inc(sem, 1))
        self._mark(tok, e.key, reads, writes, pwrites)
        return tok

    def dma(self, ek, fn, sem, reads=(), writes=(), pwrites=(), n=1):
        e = self.E[ek]
        self._deps(e, reads, writes, pwrites)
        self.dsem_n[sem] += 16 * n
        tok = ("d", sem, self.dsem_n[sem])
        hsem = self.sems[sem]

        def run(eng, fn=fn, hsem=hsem):
            r = fn(eng)
            if not isinstance(r, (list, tuple)):
                r = [r]
            for i in r:
                i.then_inc(hsem, 16)

        e.prog.append(run)
        self._mark(tok, sem, reads, writes, pwrites)
        return tok

    def wait_all(self, ek, bufs):
        e = self.E[ek]
        for b in bufs:
            for t in b.w.values():
                self._wait(e, t)
            for t in b.r.values():
                self._wait(e, t)

    def emit(self):
        nc = self.nc
        with nc.Block() as block:
            @block.tensor
            def _(eng):
                for f in self.E["pe"].prog:
                    f(eng)

            @block.scalar
            def _(eng):
                for f in self.E["act"].prog:
                    f(eng)

            @block.vector
            def _(eng):
                for f in self.E["dve"].prog:
                    f(eng)

            @block.gpsimd
            def _(eng):
                for f in self.E["pool"].prog:
                    f(eng)

            @block.sync
            def _(eng):
                for f in self.E["sp"].prog:
                    f(eng)


class Tmp:
    __slots__ = ("a", "_b", "s", "pool", "k", "gen")

    def __init__(self, t, b, s, pool, k, gen):
        self.a = t
        self._b = b
        self.s = s
        self.pool = pool
        self.k = k
        self.gen = gen

    @property
    def b(self):
        assert self.pool.gen[self.k] == self.gen, "temp reused while live"
        return self._b


class TPool:
    def __init__(self, fw, tensors, name, base=None, idxs=None):
        if base is None:
            self.t = tensors
            self.b = [Buf(f"{name}{i}") for i in range(len(tensors))]
            self.s = [fw.dsem(f"d_{name}{i}") for i in range(len(tensors))]
            self.gen = [0] * len(tensors)
            self.idxs = list(range(len(tensors)))
        else:
            self.t = base.t
            self.b = base.b
            self.s = base.s
            self.gen = base.gen
            self.idxs = list(idxs)
        self.i = 0

    def sub(self, idxs):
        return TPool(None, None, None, base=self, idxs=idxs)

    def get(self):
        k = self.idxs[self.i % len(self.idxs)]
        self.i += 1
        self.gen[k] += 1
        return Tmp(self.t[k], self.b[k], self.s[k], self, k, self.gen[k])


def build_nc(SEQ=2048, PAST=2048, do_sample=True, n_ptiles=None):
    nc = bass.Bass("TRN2", target_bir_lowering=False)
    NTILE = SEQ // TT
    NPB = PAST // 128

    def din(name, shape, dt=F32):
        return nc.dram_tensor(name, list(shape), dt, kind="ExternalInput").ap()

    def dout(name, shape, dt=F32):
        return nc.dram_tensor(name, list(shape), dt, kind="ExternalOutput").ap()

    def dscr(name, shape, dt=BF16):
        return nc.dram_tensor(name, list(shape), dt, kind="Internal").ap()

    xp = din("xp", [NSEQ, SEQ, D])
    xs = din("xs", [NSEQ, SS, D])
    st_in = din("st_in", [128, NST])
    ck = din("ck", [NSEQ, PAST, D])
    cv = din("cv", [NSEQ, PAST, D])
    prm_d = din("prm", [128, NPRM])
    ident_d = din("ident", [128, 128])
    W32 = {
        "rg_in": din("w_rg_in", [D, 2 * D]),
        "gate": din("w_gate", [DC * 128, 256]),
        "rg_out": din("w_rg_out", [D, D]),
        "up0": din("w_up0", [D, 2 * DFF]),
        "down0": din("w_down0", [DFF, D]),
        "qkv": din("w_qkv", [D, 3 * D]),
        "at_out": din("w_at_out", [D, D]),
        "up1": din("w_up1", [D, 2 * DFF]),
        "down1": din("w_down1", [DFF, D]),
    }
    WB = {"gate": dscr("wb_gate", W32["gate"].shape)}
    NSLAB = 96
    WBS = dscr("wbs", [NSLAB, 128, 16 * 512])
    ckb = dscr("ckb", [NSEQ, PAST, D])
    cvb = dscr("cvb", [NSEQ, PAST, D])
    ktT = dscr("ktT", [NSEQ, DC, 128, SEQ])
    vsc = dscr("vsc", [NSEQ, SEQ, D])

    y_p = dout("y_p", [NSEQ, SEQ, D])
    y_s = dout("y_s", [NSEQ, SS, D])
    k_p = dout("k_p", [NSEQ, SEQ, D])
    v_p = dout("v_p", [NSEQ, SEQ, D])
    k_s = dout("k_s", [NSEQ, SS, D])
    v_s = dout("v_s", [NSEQ, SS, D])
    st_p = dout("st_p", [128, NST])
    st_s = dout("st_s", [128, NST])

    stack = ExitStack()
    with stack:
        sb = lambda name, shape, dt: stack.enter_context(nc.sbuf_tensor(name, list(shape), dt))
        xT = sb("xT", [128, DC, TT], F32)
        arena = sb("arena", [128, 64 * 512], BF16)
        wsl = [sb(f"wsl{i}", [128, 16, 512], BF16) for i in range(NB)]
        tmps = [sb(f"tmp{i}", [128, TW], F32) for i in range(NTMP)]
        t16s = [sb(f"t16_{i}", [128, 512], BF16) for i in range(NT16)]
        prm = sb("prm_sb", [128, NPRM], F32)
        dp = sb("dp", [128, 128], F32)
        rstd_sb = sb("rstd_sb", [128, TT], F32)
        stb = sb("stb", [128, NST], F32)
        ident = sb("ident_sb", [128, 128], F32)
        identb = sb("identb", [128, 128], BF16)
        onesb = sb("onesb", [128, 128], BF16)
        psum = [stack.enter_context(nc.psum_tensor(f"ps{i}", [128, 512], F32)) for i in range(8)]
        fw = FW(nc, stack)

        plan = None
        for pass_ in (0, 1):
            fw.reset()
            rec = []
            _emit_all(nc, fw, locals(), plan, rec)
            plan = rec
        fw.emit()
    return nc


def _emit_all(nc, fw, env, plan, rec):
    SEQ = env["SEQ"]; PAST = env["PAST"]; NTILE = env["NTILE"]; NPB = env["NPB"]
    xp = env["xp"]; xs = env["xs"]; st_in = env["st_in"]; ck = env["ck"]; cv = env["cv"]
    prm_d = env["prm_d"]; ident_d = env["ident_d"]; W32 = env["W32"]; WB = env["WB"]; WBS = env["WBS"]; NSLAB = env["NSLAB"]
    ckb = env["ckb"]; cvb = env["cvb"]; ktT = env["ktT"]; vsc = env["vsc"]
    y_p = env["y_p"]; y_s = env["y_s"]; k_p = env["k_p"]; v_p = env["v_p"]; k_s = env["k_s"]; v_s = env["v_s"]
    st_p = env["st_p"]; st_s = env["st_s"]
    xT = env["xT"]; arena = env["arena"]; wsl = env["wsl"]; prm = env["prm"]; dp = env["dp"]; stb = env["stb"]
    ident = env["ident"]; identb = env["identb"]; onesb = env["onesb"]; psum = env["psum"]
    do_sample = env["do_sample"]; n_ptiles = env["n_ptiles"]; rstd_sb = env["rstd_sb"]
    b_rstd = Buf("rstd")

    op = fw.op
    TP = TPool(fw, env["tmps"], "tmp")
    TP16 = TPool(fw, env["t16s"], "t16")
    TP_XC = TP.sub(range(0, 5))
    TP_RB = TP.sub(range(5, 8))
    TP_G = TP.sub(range(8, 16))
    b_xT = [Buf(f"xT{c}") for c in range(DC)]
    b_slot = [Buf(f"slot{i}") for i in range(64)]
    b_ps = [Buf(f"ps{i}") for i in range(8)]
    b_prm = Buf("prm"); b_dp = Buf("dp"); b_const = Buf("const")
    b_rgc = [Buf(f"rgc{c}") for c in range(DC)]
    b_rgh = [Buf(f"rgh{c}") for c in range(DC)]
    b_ffn = [[Buf(f"ffn{l}_{j}") for j in range(2 * FC)] for l in range(2)]
    all_st = b_rgc + b_rgh + b_ffn[0] + b_ffn[1]
    b_wb = {k: Buf("wb_" + k) for k in WB}
    b_wbs = [Buf(f"wbs{i}") for i in range(NSLAB)]
    b_ckb = Buf("ckb"); b_cvb = Buf("cvb")
    b_kt = [Buf(f"ktsc{s}") for s in range(NSEQ)]
    b_vs = [Buf(f"vssc{s}") for s in range(NSEQ)]
    b_wsl = [Buf(f"wsl{i}") for i in range(NB)]
    s_wsl = [fw.dsem(f"d_wsl{i}") for i in range(NB)]
    s_misc = fw.dsem("d_misc")
    s_prm = fw.dsem("d_prm")
    s_gwb = fw.dsem("d_gwb")
    s_ident = fw.dsem("d_ident")
    s_st = fw.dsem("d_st")
    s_kv = [fw.dsem(f"d_kv{i}") for i in range(4)]
    pool_ok = [False]
    si_ = [0]

    def EP():
        return "pool" if pool_ok[0] else "dve"

    def av(off, shape):
        n = int(np.prod(shape))
        v = arena[:, off:off + n]
        if len(shape) == 2:
            v = v.rearrange("p (a b) -> p a b", b=shape[1])
        bufs = b_slot[off // 512:(off + n - 1) // 512 + 1]
        return v, bufs

    def slot(i):
        return arena[:, i * 512:(i + 1) * 512]

    ps_i = [0]

    def ps_next():
        k = ps_i[0] % 8
        ps_i[0] += 1
        return k

    rgc_v = stb[:, S_RGC:S_RGC + 96].rearrange("p (c b k) -> p c b k", b=2, k=3)
    rgh_v = stb[:, S_RGH:S_RGH + 32].rearrange("p (c b) -> p c b", b=2)
    ffn_v = stb[:, S_FFN:S_FFN + 704].rearrange("p (l j b k) -> p l j b k", j=88, b=2, k=2)

    def pcol(off, i=0):
        return prm[:, off + i:off + i + 1]

    fw.dma("sp", lambda e: e.dma_start(out=prm[:], in_=prm_d[:]), s_prm, writes=[b_prm])
    fw.dma("sp", lambda e: e.dma_start(out=ident[:], in_=ident_d[:]), s_ident, writes=[b_const])
    op("dve", lambda e: e.tensor_copy(out=identb[:], in_=ident[:]), reads=[b_const], pwrites=[b_const])
    op("dve", lambda e: e.memset(onesb[:], 1.0), pwrites=[b_const])
    op("act", lambda e: e.activation(out=dp[:, 0:16], in_=prm[:, P_LL:P_LL + 16], func=AF.Exp, scale=-1.0),
       reads=[b_prm], writes=[b_dp])
    op("act", lambda e: e.activation(out=dp[:, 0:16], in_=dp[:, 0:16], func=AF.Ln, bias=1.0), writes=[b_dp])
    op("dve", lambda e: e.tensor_scalar(out=dp[:, 16:32], in0=dp[:, 0:16], scalar1=-0.5 * RG_C, scalar2=None,
                                        op0=ALU.mult), reads=[b_dp], pwrites=[b_dp])
    op("dve", lambda e: e.tensor_scalar(out=dp[:, 0:16], in0=dp[:, 0:16], scalar1=-RG_C, scalar2=None,
                                        op0=ALU.mult), writes=[b_dp])
    op("dve", lambda e: e.tensor_scalar(out=dp[:, 64:96], in0=prm[:, P_RGGB:P_RGGB + 32], scalar1=0.5, scalar2=None,
                                        op0=ALU.mult), reads=[b_prm], writes=[b_dp])
    op("dve", lambda e: e.tensor_scalar(out=dp[:, 32:34], in0=prm[:, P_SUB:P_SUB + 2], scalar1=1.0 - LAM_INIT,
                                        scalar2=None, op0=ALU.mult), reads=[b_prm], writes=[b_dp])
    lt = TP16.get()
    op("dve", lambda e: e.tensor_tensor(out=lt.a[:, 0:1], in0=pcol(P_LAM, 0), in1=pcol(P_LAM, 1), op=ALU.mult),
       reads=[b_prm], writes=[lt.b])
    op("dve", lambda e: e.tensor_tensor(out=lt.a[:, 1:2], in0=pcol(P_LAM, 2), in1=pcol(P_LAM, 3), op=ALU.mult),
       reads=[b_prm], writes=[lt.b])
    k = ps_next()
    op("pe", lambda e: e.matmul(psum[k][:, 0:2], lhsT=onesb[:], rhs=lt.a[:, 0:2], start=True, stop=True),
       reads=[lt.b, b_const], writes=[b_ps[k]])
    op("act", lambda e: e.activation(out=dp[:, 40:42], in_=psum[k][:, 0:2], func=AF.Exp), reads=[b_ps[k]], writes=[b_dp])
    op("dve", lambda e: e.tensor_tensor(out=dp[:, 34:35], in0=dp[:, 41:42], in1=dp[:, 40:41], op=ALU.subtract),
       writes=[b_dp])
    op("dve", lambda e: e.tensor_scalar(out=dp[:, 34:35], in0=dp[:, 34:35], scalar1=-LAM_INIT, scalar2=None,
                                        op0=ALU.add), writes=[b_dp])
    CL = lambda c: dp[:, c:c + 1]
    HCL = lambda c: dp[:, 16 + c:17 + c]
    HGB = lambda i: dp[:, 64 + i:65 + i]
    GS = lambda e_: dp[:, 32 + e_:33 + e_]
    NEGLAM = dp[:, 34:35]

    wi = [0]
    issued = [0]

    cur_first = [True]
    s_wbk = [fw.dsem(f"d_wbk{i}") for i in range(NB)]

    def _issue(j):
        spec, first = plan[j]
        kname, k0, nk, c0, ncol = spec
        idx = j % NSLAB
        assert plan[idx][0] == spec
        src32 = W32[kname][k0 * 128:(k0 + nk) * 128, c0:c0 + ncol].rearrange("(kc p) n -> p kc n", p=128)
        scr = WBS[idx][:, 0:nk * ncol].rearrange("p (kc n) -> p kc n", n=ncol)
        bi = j % NB
        dst = wsl[bi][:, 0:nk, 0:ncol]
        if first:
            fw.dma("pool", lambda e: e.dma_start(out=dst, in_=src32), s_wsl[bi], writes=[b_wsl[bi]])
            fw.dma("sp", lambda e: e.dma_start(out=scr, in_=dst), s_wbk[bi], reads=[b_wsl[bi]], writes=[b_wbs[idx]])
        else:
            fw.dma("sp", lambda e: e.dma_start(out=dst, in_=scr), s_wsl[bi], reads=[b_wbs[idx]], writes=[b_wsl[bi]])

    released = {}
    held = set()

    def _try_issue():
        if plan is None:
            return
        lim = min(wi[0] - 1 + NB, len(plan))
        while issued[0] < lim:
            j = issued[0]
            if j - NB >= 0 and not released.get(j - NB, False):
                break
            _issue(j)
            issued[0] += 1

    def slab(spec, hold=False):
        j = wi[0]
        wi[0] += 1
        rec.append((spec, cur_first[0]))
        for q in range(j):
            if q not in held:
                released[q] = True
        if hold:
            held.add(j)
        if plan is not None:
            assert plan[j][0] == spec, (plan[j], spec)
        _try_issue()
        if plan is not None:
            assert issued[0] > j, "slab not issued (buffer still held)"
        return wsl[j % NB], b_wsl[j % NB], j

    def slab_release(j):
        held.discard(j)
        released[j] = True
        _try_issue()

    def rmsnorm(T, goff, presq=False):
        for c in range(0 if not presq else DC, DC):
            op("act", lambda e, c=c: e.activation(out=slot(c)[:, 0:T], in_=xT[:, c, 0:T], func=AF.Square),
               reads=[b_xT[c]], writes=[b_slot[c]])
        k = ps_next()

        def burst(e):
            for c in range(DC):
                i = e.matmul(psum[k][:, 0:T], lhsT=onesb[:], rhs=slot(c)[:, 0:T], start=(c == 0), stop=(c == DC - 1))
            return i

        op("pe", burst, reads=b_slot[0:DC] + [b_const], writes=[b_ps[k]])
        rt = TP.get()
        op("act", lambda e: e.activation(out=rt.a[:, 0:T], in_=psum[k][:, 0:T], func=AF.Sqrt, scale=1.0 / D, bias=EPS),
           reads=[b_ps[k]], writes=[rt.b])
        op("dve", lambda e: e.reciprocal(out=rt.a[:, 0:T], in_=rt.a[:, 0:T]), writes=[rt.b])
        for c in range(DC):
            op("dve", lambda e, c=c: e.scalar_tensor_tensor(out=slot(c)[:, 0:T], in0=xT[:, c, 0:T], scalar=pcol(goff, c),
                                                           in1=rt.a[:, 0:T], op0=ALU.mult, op1=ALU.mult),
               reads=[b_xT[c], rt.b, b_prm], writes=[b_slot[c]])

    def proj_A(kname, cgs, K_chunks, rhs_fn, T, evac, fine=True):
        cgs = list(cgs)
        for cg in cgs:
            banks = [ps_next() for _ in range(4)]
            kgs = [(k0, min(16, K_chunks - k0)) for k0 in range(0, K_chunks, 16)]
            ng = len(kgs)
            for gi, (k0, nk) in enumerate(kgs):
                wt, wbuf, _ = slab((kname, k0, nk, cg * 512, 512))
                rhs = [rhs_fn(k0 + kc) for kc in range(nk)]
                if fine and cg == cgs[0] and gi == 0:
                    for kc in range(nk):
                        for oc in range(4):
                            def one(e, oc=oc, wt=wt, nk=nk, kc=kc, bk=banks[oc], rhs=rhs, ng=ng):
                                return e.matmul(psum[bk][:, 0:T], lhsT=wt[:, kc, oc * 128:(oc + 1) * 128],
                                                rhs=rhs[kc][0][:, 0:T], start=(kc == 0), stop=(ng == 1 and kc == nk - 1))
                            if kc == 0:
                                op("pe", one, reads=[wbuf, rhs[kc][1]], writes=[b_ps[banks[oc]]])
                            else:
                                op("pe", one, reads=[wbuf, rhs[kc][1]], pwrites=[b_ps[banks[oc]]])
                    continue
                for oc in range(4):
                    def burst(e, oc=oc, wt=wt, nk=nk, gi=gi, bk=banks[oc], rhs=rhs, ng=ng):
                        for kc in range(nk):
                            i = e.matmul(psum[bk][:, 0:T], lhsT=wt[:, kc, oc * 128:(oc + 1) * 128],
                                         rhs=rhs[kc][0][:, 0:T],
                                         start=(gi == 0 and kc == 0), stop=(gi == ng - 1 and kc == nk - 1))
                        return i
                    rd = [wbuf] + [r[1] for r in rhs]
                    if gi == 0:
                        op("pe", burst, reads=rd, writes=[b_ps[banks[oc]]])
                    else:
                        op("pe", burst, reads=rd, pwrites=[b_ps[banks[oc]]])
            for oc in range(4):
                evac(cg * 4 + oc, banks[oc])

    def xn_rhs(kc):
        return slot(kc), b_slot[kc]

    def s16_rhs(kc):
        return slot(16 + kc), b_slot[16 + kc]

    def resid_add(T, sq=False):
        def ev(ocg, bk):
            op("dve", lambda e: e.tensor_tensor(out=xT[:, ocg, 0:T], in0=psum[bk][:, 0:T], in1=xT[:, ocg, 0:T], op=ALU.add),
               reads=[b_ps[bk]], writes=[b_xT[ocg]])
            if sq:
                op("act", lambda e: e.activation(out=slot(ocg)[:, 0:T], in_=xT[:, ocg, 0:T], func=AF.Square),
                   reads=[b_xT[ocg]], writes=[b_slot[ocg]])
        return ev

    def tile(kind, s, ti):
        prompt = kind == "p"
        T = TT if prompt else NSEQ * SS
        segs = [(0, TT, s)] if prompt else [(b * SS, SS, b) for b in range(NSEQ)]

        def pad0(si, hw):
            return sum(l + hw for (_, l, _) in segs[:si])

        if prompt:
            def qt(m):
                return slot(16 + m), b_slot[16 + m]

            def on(m):
                return slot(32 + m), b_slot[32 + m]
        else:
            def qt(m):
                off = 16 * 512 + m * 64
                return arena[:, off:off + 64], b_slot[off // 512]

            def on(m):
                off = 18 * 512 + m * 64
                return arena[:, off:off + 64], b_slot[off // 512]

        def std_bursts(wt, wbuf, banks, fine=False):
            if fine:
                for kc in range(16):
                    for oc in range(4):
                        def one(e, oc=oc, kc=kc, bk=banks[oc]):
                            return e.matmul(psum[bk][:, 0:T], lhsT=wt[:, kc, oc * 128:(oc + 1) * 128], rhs=slot(kc)[:, 0:T],
                                            start=(kc == 0), stop=(kc == 15))
                        if kc == 0:
                            op("pe", one, reads=[wbuf, b_slot[kc]], writes=[b_ps[banks[oc]]])
                        else:
                            op("pe", one, reads=[wbuf, b_slot[kc]], pwrites=[b_ps[banks[oc]]])
                return
            for oc in range(4):
                def burst(e, oc=oc, bk=banks[oc]):
                    for kc in range(16):
                        i = e.matmul(psum[bk][:, 0:T], lhsT=wt[:, kc, oc * 128:(oc + 1) * 128], rhs=slot(kc)[:, 0:T],
                                     start=(kc == 0), stop=(kc == 15))
                    return i
                op("pe", burst, reads=[wbuf] + b_slot[0:16], writes=[b_ps[banks[oc]]])

        def load_x(cg):
            xin = []
            if prompt:
                for tb in range(4):
                    t = TP.get()
                    src = xp[s, ti * TT + tb * 128:ti * TT + (tb + 1) * 128, cg * 512:(cg + 1) * 512]
                    fw.dma("sp", lambda e, t=t, src=src: e.dma_start(out=t.a[:, 0:512], in_=src), t.s, writes=[t.b])
                    xin.append((t, 128, tb * 128))
            else:
                t = TP.get()

                def go(e, t=t):
                    return [e.dma_start(out=t.a[b * SS:(b + 1) * SS, 0:512], in_=xs[b, :, cg * 512:(cg + 1) * 512]) for b in range(NSEQ)]
                fw.dma("sp", go, t.s, writes=[t.b], n=NSEQ)
                xin.append((t, NSEQ * SS, 0))
            for cl in range(4):
                c = cg * 4 + cl
                k = ps_next()

                def burst(e, k=k, cl=cl):
                    for (t, ntok, col0) in xin:
                        i = e.transpose(out=psum[k][:, col0:col0 + ntok], in_=t.a[0:ntok, cl * 128:(cl + 1) * 128],
                                        identity=ident[0:ntok, 0:ntok])
                    return i
                op("pe", burst, reads=[t.b for (t, _, _) in xin] + [b_const], writes=[b_ps[k]])
                op("act", lambda e, k=k, c=c: e.activation(out=xT[:, c, 0:T], in_=psum[k][:, 0:T], func=AF.Identity),
                   reads=[b_ps[k]], writes=[b_xT[c]])

        for cg in range(4):
            load_x(cg)

        rmsnorm(T, P_RGN)
        HY0 = 16
        chain = {}
        gate_v, gate_b = av(32 * 512, (16, 256))
        if cur_first[0]:
            fw.dma("pool", lambda e: e.dma_start(out=gate_v, in_=W32["gate"].rearrange("(n c) d -> c n d", c=128)),
                   s_misc, writes=gate_b)
            fw.dma("sp", lambda e: e.dma_start(out=WB["gate"].rearrange("(n c) d -> c n d", c=128), in_=gate_v),
                   s_gwb, reads=gate_b, pwrites=[b_wb["gate"]])
        else:
            fw.dma("sp", lambda e: e.dma_start(out=gate_v, in_=WB["gate"].rearrange("(n c) d -> c n d", c=128)),
                   s_misc, reads=[b_wb["gate"]], writes=gate_b)

        def rg_rec1(n, bk):
            rb = TP_RB.get()
            xc = TP_XC.get()
            cw = lambda kk: pcol(P_RGCW, n * 4 + kk)
            for si, (c0, ln, b) in enumerate(segs):
                p0 = pad0(si, 3)
                op(EP(), lambda e, p0=p0, b=b: e.tensor_copy(out=rb.a[:, p0:p0 + 3], in_=rgc_v[:, n, b, :]),
                   reads=[b_rgc[n]], writes=[rb.b])
                op("act", lambda e, p0=p0, c0=c0, ln=ln: e.activation(out=rb.a[:, p0 + 3:p0 + 3 + ln],
                                                                      in_=psum[bk][:, c0:c0 + ln], func=AF.Identity),
                   reads=[b_ps[bk]], writes=[rb.b])
            for si, (c0, ln, b) in enumerate(segs):
                p0 = pad0(si, 3)
                op(EP(), lambda e, p0=p0, b=b, ln=ln: e.tensor_copy(out=rgc_v[:, n, b, :], in_=rb.a[:, p0 + ln:p0 + ln + 3]),
                   reads=[rb.b], writes=[b_rgc[n]])
                op("dve", lambda e, p0=p0, c0=c0, ln=ln: e.tensor_scalar(
                    out=xc.a[:, c0:c0 + ln], in0=rb.a[:, p0:p0 + ln], scalar1=cw(0), scalar2=pcol(P_RGCB, n),
                    op0=ALU.mult, op1=ALU.add), reads=[rb.b, b_prm], writes=[xc.b])
                for kk in (1, 2, 3):
                    op("dve", lambda e, p0=p0, c0=c0, ln=ln, kk=kk: e.scalar_tensor_tensor(
                        out=xc.a[:, c0:c0 + ln], in0=rb.a[:, p0 + kk:p0 + kk + ln], scalar=cw(kk), in1=xc.a[:, c0:c0 + ln],
                        op0=ALU.mult, op1=ALU.add), reads=[rb.b, b_prm], writes=[xc.b])
            chain[n] = [xc, None]

        def rg_rec2(n):
            xc = chain[n][0]
            xb = TP16.get()
            op(EP(), lambda e: e.tensor_copy(out=xb.a[:, 0:T], in_=xc.a[:, 0:T]), reads=[xc.b], writes=[xb.b])
            chain[n][1] = xb

        RBK = [0, 1, 2, 3]
        GBK = [4, 5]
        KBK = [6, 7]
        slabs_g = {}
        slabs_r = {}

        def one_burst(wt, wbuf, oc, bk):
            def burst(e):
                for kc in range(16):
                    i = e.matmul(psum[bk][:, 0:T], lhsT=wt[:, kc, oc * 128:(oc + 1) * 128], rhs=slot(kc)[:, 0:T],
                                 start=(kc == 0), stop=(kc == 15))
                return i
            op("pe", burst, reads=[wbuf] + b_slot[0:16], writes=[b_ps[bk]])

        def g_burst(n):
            kg, oc = n // 4, n % 4
            if oc == 0:
                slabs_g[kg] = slab(("rg_in", 0, 16, kg * 512, 512), hold=True)
            wt, wbuf, sj = slabs_g[kg]
            bk = GBK[n % 2]
            one_burst(wt, wbuf, oc, bk)
            if oc == 3:
                slab_release(sj)
            op("act", lambda e: e.activation(out=slot(HY0 + n)[:, 0:T], in_=psum[bk][:, 0:T], func=AF.Gelu_apprx_tanh),
               reads=[b_ps[bk]], writes=[b_slot[HY0 + n]])

        def r_burst(n):
            kg, oc = n // 4, n % 4
            if oc == 0:
                slabs_r[kg] = slab(("rg_in", 0, 16, D + kg * 512, 512), hold=True)
            wt, wbuf, sj = slabs_r[kg]
            one_burst(wt, wbuf, oc, RBK[n % 4])
            if oc == 3:
                slab_release(sj)

        backs = {}

        def rg_backA(n):
            xc, xb = chain.pop(n)
            kr = KBK[0]
            ki = KBK[1]
            op("pe", lambda e: e.matmul(psum[kr][:, 0:T], lhsT=gate_v[:, n, 0:128], rhs=xb.a[:, 0:T], start=True, stop=True),
               reads=list(gate_b) + [xb.b], writes=[b_ps[kr]])
            op("pe", lambda e: e.matmul(psum[ki][:, 0:T], lhsT=gate_v[:, n, 128:256], rhs=xb.a[:, 0:T], start=True, stop=True),
               reads=list(gate_b) + [xb.b], writes=[b_ps[ki]])
            r = TP_G.get()
            it = TP_G.get()
            a = TP_G.get()
            m = TP_G.get()
            op("act", lambda e: e.activation(out=r.a[:, 0:T], in_=psum[kr][:, 0:T], func=AF.Tanh, scale=0.5, bias=HGB(n * 2)),
               reads=[b_ps[kr], b_dp], writes=[r.b])
            op("act", lambda e: e.activation(out=it.a[:, 0:T], in_=psum[ki][:, 0:T], func=AF.Tanh, scale=0.5, bias=HGB(n * 2 + 1)),
               reads=[b_ps[ki], b_dp], writes=[it.b])
            op("act", lambda e: e.activation(out=a.a[:, 0:T], in_=r.a[:, 0:T], func=AF.Exp, scale=HCL(n), bias=HCL(n)),
               reads=[r.b, b_dp], writes=[a.b])
            op("act", lambda e: e.activation(out=m.a[:, 0:T], in_=r.a[:, 0:T], func=AF.Exp, scale=CL(n), bias=CL(n)),
               reads=[r.b, b_dp], writes=[m.b])
            op("act", lambda e: e.activation(out=m.a[:, 0:T], in_=m.a[:, 0:T], func=AF.Sqrt, scale=-0.25, bias=0.25),
               writes=[m.b])
            op("dve", lambda e: e.scalar_tensor_tensor(out=it.a[:, 0:T], in0=it.a[:, 0:T], scalar=1.0, in1=m.a[:, 0:T],
                                                       op0=ALU.add, op1=ALU.mult), reads=[m.b], writes=[it.b])
            op(EP(), lambda e: e.tensor_tensor(out=m.a[:, 0:T], in0=it.a[:, 0:T], in1=xc.a[:, 0:T], op=ALU.mult),
               reads=[it.b, xc.b], writes=[m.b])
            backs[n] = (r, a, m)

        def rg_backB(n):
            h, a, m = backs.pop(n)
            for si, (c0, ln, b) in enumerate(segs):
                op("dve", lambda e, c0=c0, ln=ln, b=b: e.tensor_tensor_scan(
                    out=h.a[:, c0:c0 + ln], data0=a.a[:, c0:c0 + ln], data1=m.a[:, c0:c0 + ln], initial=rgh_v[:, n, b:b + 1],
                    op0=ALU.mult, op1=ALU.add), reads=[a.b, m.b, b_rgh[n]], writes=[h.b])
                op(EP(), lambda e, c0=c0, ln=ln, b=b: e.tensor_copy(out=rgh_v[:, n, b:b + 1], in_=h.a[:, c0 + ln - 1:c0 + ln]),
                   reads=[h.b], writes=[b_rgh[n]])
            op(EP(), lambda e: e.tensor_tensor(out=slot(HY0 + n)[:, 0:T], in0=h.a[:, 0:T], in1=slot(HY0 + n)[:, 0:T], op=ALU.mult),
               reads=[h.b], writes=[b_slot[HY0 + n]])

        r_burst(0)
        for i_ in range(16 + 5):
            if 0 <= i_ - 4 < 16:
                rg_backB(i_ - 4)
            if 0 <= i_ - 3 < 16:
                rg_backA(i_ - 3)
            if i_ < 16:
                g_burst(i_)
            if i_ + 1 < 16:
                r_burst(i_ + 1)
            if i_ < 16:
                rg_rec1(i_, RBK[i_ % 4])
                rg_rec2(i_)
        proj_A("rg_out", range(4), 16, s16_rhs, T, resid_add(T, sq=True))

        def ffn(l):
            rmsnorm(T, P_FFN + l * 16, presq=True)
            H0 = 16
            up = "up%d" % l
            pend = {}

            prev = [None]

            def up_s12(j, bk):
                ub = TP.get()
                cc = TP.get()
                fcw = lambda kk: pcol(P_FCW, (l * 88 + j) * 3 + kk)
                for si, (c0, ln, b) in enumerate(segs):
                    p0 = pad0(si, 2)
                    op(EP(), lambda e, p0=p0, b=b: e.tensor_copy(out=ub.a[:, p0:p0 + 2], in_=ffn_v[:, l, j, b, :]),
                       reads=[b_ffn[l][j]], writes=[ub.b])
                    op("act", lambda e, p0=p0, c0=c0, ln=ln: e.activation(
                        out=ub.a[:, p0 + 2:p0 + 2 + ln], in_=psum[bk][:, c0:c0 + ln], func=AF.Identity),
                       reads=[b_ps[bk]], writes=[ub.b])
                op("act", lambda e: e.activation(out=cc.a[:, 0:T], in_=psum[bk][:, 0:T], func=AF.Identity,
                                                 scale=fcw(2), bias=pcol(P_FCB, l * 88 + j)),
                   reads=[b_ps[bk], b_prm], writes=[cc.b])
                for si, (c0, ln, b) in enumerate(segs):
                    p0 = pad0(si, 2)
                    op(EP(), lambda e, p0=p0, b=b, ln=ln: e.tensor_copy(out=ffn_v[:, l, j, b, :], in_=ub.a[:, p0 + ln:p0 + ln + 2]),
                       reads=[ub.b], writes=[b_ffn[l][j]])
                    for kk in (1, 0):
                        op("dve", lambda e, p0=p0, c0=c0, ln=ln, kk=kk: e.scalar_tensor_tensor(
                            out=cc.a[:, c0:c0 + ln], in0=ub.a[:, p0 + kk:p0 + kk + ln], scalar=fcw(kk), in1=cc.a[:, c0:c0 + ln],
                            op0=ALU.mult, op1=ALU.add), reads=[ub.b, b_prm], writes=[cc.b])
                return cc

            def up_s3(j, cc):
                if j < FC:
                    op("act", lambda e: e.activation(out=cc.a[:, 0:T], in_=cc.a[:, 0:T], func=AF.Gelu_apprx_tanh), writes=[cc.b])
                    pend[j] = cc
                else:
                    g = pend.pop(j - FC)
                    op(EP(), lambda e: e.tensor_tensor(out=slot(H0 + j - FC)[:, 0:T], in0=g.a[:, 0:T], in1=cc.a[:, 0:T], op=ALU.mult),
                       reads=[g.b, cc.b], writes=[b_slot[H0 + j - FC]])

            def up_slab(cgi):
                banks = [ps_next() for _ in range(4)]
                wt, wbuf, _ = slab((up, 0, 16, cgi * 512, 512))
                std_bursts(wt, wbuf, banks, fine=(cgi == 0))
                for oc in range(4):
                    j = cgi * 4 + oc
                    cc = up_s12(j, banks[oc])
                    if prev[0] is not None:
                        up_s3(*prev[0])
                    prev[0] = (j, cc)

            for kk in range(11):
                up_slab(kk)
                up_slab(11 + kk)
            up_s3(*prev[0])
            proj_A("down%d" % l, range(4), FC, s16_rhs, T, resid_add(T, sq=True))

        ffn(0)

        rmsnorm(T, P_ATN, presq=True)
        ktn_v, ktn_b = (None, None)
        vn_v, vn_b = (None, None)
        if not prompt:
            ktn_v, ktn_b = av(20 * 512, (16, 64))
            vn_v, vn_b = av(22 * 512, (2, 2048))

        tokblocks = [(tb * 128, 128) for tb in range(4)] if prompt else [(b * SS, SS) for b in range(NSEQ)]

        def out_rows(dst_p, dst_s, bi):
            if prompt:
                return dst_p[s, ti * TT + bi * 128:ti * TT + (bi + 1) * 128, :]
            return dst_s[bi]

        def bform(wt, wbuf, cgx, dst_p, dst_s, extra):
            for bi, (c0, ntok) in enumerate(tokblocks):
                k = ps_next()

                def burst(e, k=k, c0=c0, ntok=ntok):
                    for kc in range(16):
                        i = e.matmul(psum[k][0:ntok, :], lhsT=slot(kc)[:, c0:c0 + ntok], rhs=wt[:, kc, :], start=(kc == 0), stop=(kc == 15))
                    return i
                op("pe", burst, reads=[wbuf] + b_slot[0:16], writes=[b_ps[k]])
                ot = TP.get()
                op("act", lambda e, k=k, ntok=ntok, ot=ot: e.activation(out=ot.a[0:ntok, 0:512], in_=psum[k][0:ntok, :], func=AF.Identity),
                   reads=[b_ps[k]], writes=[ot.b])
                dst = out_rows(dst_p, dst_s, bi)[:, cgx * 512:(cgx + 1) * 512]
                fw.dma("sp", lambda e, ot=ot, ntok=ntok, dst=dst: e.dma_start(out=dst, in_=ot.a[0:ntok, 0:512]),
                       ot.s, reads=[ot.b])
                if extra is not None:
                    extra(ot, bi, ntok)

        def k_evac(m, bk):
            kf = TP.get()
            op("act", lambda e: e.activation(out=kf.a[:, 0:T], in_=psum[bk][:, 0:T], func=AF.Identity),
               reads=[b_ps[bk]], writes=[kf.b])
            if prompt:
                kt = TP16.get()
                op(EP(), lambda e: e.tensor_copy(out=kt.a[:, 0:T], in_=kf.a[:, 0:T]), reads=[kf.b], writes=[kt.b])
                fw.dma("sp", lambda e: e.dma_start(out=ktT[s, m, :, ti * TT:(ti + 1) * TT], in_=kt.a[:, 0:T]),
                       kt.s, reads=[kt.b], pwrites=[b_kt[s]])
            else:
                op(EP(), lambda e: e.tensor_copy(out=ktn_v[:, m, :], in_=kf.a[:, 0:T]), reads=[kf.b], pwrites=ktn_b)
            return kf

        def k_out(cgk, kfs):
            for bi, (c0, ntok) in enumerate(tokblocks):
                k = ps_next()

                def burst(e, k=k, c0=c0, ntok=ntok):
                    for oc in range(4):
                        i = e.transpose(out=psum[k][0:ntok, oc * 128:(oc + 1) * 128], in_=kfs[oc].a[:, c0:c0 + ntok], identity=ident[:])
                    return i
                op("pe", burst, reads=[x.b for x in kfs] + [b_const], writes=[b_ps[k]])
                ot = TP.get()
                op("act", lambda e, k=k, ntok=ntok, ot=ot: e.activation(out=ot.a[0:ntok, 0:512], in_=psum[k][0:ntok, :], func=AF.Identity),
                   reads=[b_ps[k]], writes=[ot.b])
                dst = out_rows(k_p, k_s, bi)[:, cgk * 512:(cgk + 1) * 512]
                fw.dma("sp", lambda e, ot=ot, ntok=ntok, dst=dst: e.dma_start(out=dst, in_=ot.a[0:ntok, 0:512]),
                       ot.s, reads=[ot.b])

        for cgk in range(4):
            wt, wbuf, _ = slab(("qkv", 0, 16, D + cgk * 512, 512))
            banks = [ps_next() for _ in range(4)]
            std_bursts(wt, wbuf, banks, fine=(cgk == 0))
            kfs = [k_evac(cgk * 4 + oc, banks[oc]) for oc in range(4)]
            k_out(cgk, kfs)

        def v_extra_fn(cgv):
            def v_extra(ot, bi, ntok):
                if prompt:
                    vt = TP16.get()
                    op(EP(), lambda e: e.tensor_copy(out=vt.a[:, :], in_=ot.a[:, 0:512]), reads=[ot.b], writes=[vt.b])
                    r0 = ti * TT + bi * 128
                    fw.dma("sp", lambda e: e.dma_start(out=vsc[s, r0:r0 + 128, cgv * 512:(cgv + 1) * 512], in_=vt.a[:, :]),
                           vt.s, reads=[vt.b], pwrites=[b_vs[s]])
                else:
                    op(EP(), lambda e: e.tensor_copy(out=vn_v[0:ntok, bi, cgv * 512:(cgv + 1) * 512], in_=ot.a[0:ntok, 0:512]),
                       reads=[ot.b], pwrites=vn_b)
            return v_extra

        for cgv in range(4):
            wt, wbuf, _ = slab(("qkv", 0, 16, 2 * D + cgv * 512, 512))
            bform(wt, wbuf, cgv, v_p, v_s, v_extra_fn(cgv))

        def evq(ocg, bk):
            qa, qb = qt(ocg)
            op("act", lambda e: e.activation(out=qa[:, 0:T], in_=psum[bk][:, 0:T], func=AF.Identity),
               reads=[b_ps[bk]], pwrites=[qb])
        proj_A("qkv", range(4), 16, xn_rhs, T, evq)

        OB = [[0, 1, 2], [0, 1, 2]]
        SB = [3, 4, 5, 6, 7]
        LOOK = 4

        pend_tail = [None]

        def attn_unit(h, qc0, qn, kblocks):
            on_t = {}
            nkb = len(kblocks)

            def one_map(j):
                pts = {}
                qa, qb = qt(2 * h + j)
                ob = OB[j]

                def emit_S(kb):
                    ktf, vap, vbufs, nk, qoff, diag = kblocks[kb]
                    ktap, ktbufs = ktf(j)
                    sbk = SB[si_[0] % len(SB)]
                    si_[0] += 1
                    op("pe", lambda e: e.matmul(psum[sbk][0:nk, qoff:qn], lhsT=ktap, rhs=qa[:, qc0 + qoff:qc0 + qn], start=True, stop=True),
                       reads=list(ktbufs) + [qb], writes=[b_ps[sbk]])
                    pt = TP16.get()
                    op("act", lambda e: e.activation(out=pt.a[0:nk, qoff:qn], in_=psum[sbk][0:nk, qoff:qn], func=AF.Exp, scale=QSCALE),
                       reads=[b_ps[sbk]], writes=[pt.b])
                    if diag:
                        op(EP(), lambda e: e.memset(pt.a[64:128, qoff:qoff + 64], 0.0), writes=[pt.b])
                    pts[kb] = pt

                def emit_PV(kb):
                    ktf, vap, vbufs, nk, qoff, diag = kblocks[kb]
                    pt = pts.pop(kb)
                    first = kb == 0
                    last = kb == nkb - 1

                    def burst(e):
                        e.matmul(psum[ob[0]][:, qoff:qn], lhsT=vap[0:nk, 0:128], rhs=pt.a[0:nk, qoff:qn], start=first, stop=last)
                        e.matmul(psum[ob[1]][:, qoff:qn], lhsT=vap[0:nk, 128:256], rhs=pt.a[0:nk, qoff:qn], start=first, stop=last)
                        return e.matmul(psum[ob[2]][:, qoff:qn], lhsT=onesb[0:nk, :], rhs=pt.a[0:nk, qoff:qn], start=first, stop=last)
                    if first:
                        op("pe", burst, reads=list(vbufs) + [pt.b, b_const], writes=[b_ps[x] for x in ob])
                    else:
                        op("pe", burst, reads=list(vbufs) + [pt.b, b_const], pwrites=[b_ps[x] for x in ob])

                for kb in range(min(LOOK, nkb)):
                    emit_S(kb)
                for kb in range(nkb):
                    if kb + LOOK < nkb:
                        emit_S(kb + LOOK)
                    emit_PV(kb)
                ri = TP.get()
                op("act", lambda e: e.activation(out=ri.a[:, 0:qn], in_=psum[ob[2]][:, 0:qn], func=AF.Ln), reads=[b_ps[ob[2]]], writes=[ri.b])
                tjs = []
                for e_ in range(2):
                    tj = TP.get()
                    op("dve", lambda e, tj=tj, e_=e_: e.tensor_copy(out=tj.a[:, 0:qn], in_=psum[ob[e_]][:, 0:qn]),
                       reads=[b_ps[ob[e_]]], writes=[tj.b])
                    tjs.append(tj)
                op("act", lambda e: e.activation(out=ri.a[:, 0:qn], in_=ri.a[:, 0:qn], func=AF.Exp, scale=-1.0), writes=[ri.b])
                for e_ in range(2):
                    tj = tjs[e_]
                    op("dve", lambda e, tj=tj: e.tensor_tensor(out=tj.a[:, 0:qn], in0=tj.a[:, 0:qn], in1=ri.a[:, 0:qn], op=ALU.mult),
                       reads=[ri.b], writes=[tj.b])
                    on_t[(j, e_)] = tj

            one_map(0)
            if pend_tail[0] is not None:
                pend_tail[0]()
                pend_tail[0] = None
            one_map(1)
            pend_tail[0] = lambda: attn_tail(h, qc0, qn, on_t)

        def attn_tail(h, qc0, qn, on_t):
            sq = []
            for e_ in range(2):
                t0 = on_t[(0, e_)]
                t1 = on_t[(1, e_)]
                op("dve", lambda e, t0=t0, t1=t1: e.scalar_tensor_tensor(out=t0.a[:, 0:qn], in0=t1.a[:, 0:qn], scalar=NEGLAM, in1=t0.a[:, 0:qn],
                                                                        op0=ALU.mult, op1=ALU.add), reads=[t1.b, b_dp], writes=[t0.b])
                q_ = TP16.get()
                op("act", lambda e, t0=t0, q_=q_: e.activation(out=q_.a[:, 0:qn], in_=t0.a[:, 0:qn], func=AF.Square), reads=[t0.b], writes=[q_.b])
                sq.append(q_)
            sbk = SB[si_[0] % len(SB)]
            si_[0] += 1

            def burst(e):
                e.matmul(psum[sbk][:, 0:qn], lhsT=onesb[:], rhs=sq[0].a[:, 0:qn], start=True, stop=False)
                return e.matmul(psum[sbk][:, 0:qn], lhsT=onesb[:], rhs=sq[1].a[:, 0:qn], start=False, stop=True)
            op("pe", burst, reads=[sq[0].b, sq[1].b, b_const], writes=[b_ps[sbk]])
            rs = TP.get()
            op("act", lambda e: e.activation(out=rs.a[:, 0:qn], in_=psum[sbk][:, 0:qn], func=AF.Ln, scale=1.0 / (2 * HD), bias=EPS),
               reads=[b_ps[sbk]], writes=[rs.b])
            op("act", lambda e: e.activation(out=rs.a[:, 0:qn], in_=rs.a[:, 0:qn], func=AF.Exp, scale=-0.5), writes=[rs.b])
            for e_ in range(2):
                t0 = on_t[(0, e_)]
                oa, obuf = on(2 * h + e_)
                op("dve", lambda e, t0=t0, e_=e_, oa=oa: e.scalar_tensor_tensor(out=oa[:, qc0:qc0 + qn], in0=t0.a[:, 0:qn], scalar=GS(e_),
                                                                             in1=rs.a[:, 0:qn], op0=ALU.mult, op1=ALU.mult),
                   reads=[t0.b, rs.b, b_dp], pwrites=[obuf])

        if prompt:
            nkb = 4 * (ti + 1)
            nkeys = 128 * nkb
            kvsets = [(0, 8), (48, 56)]

            def p_head(h):
                s0, s1 = kvsets[h % 2]
                ktv, ktb = av(s0 * 512, (2, 2048))
                vv, vb = av(s1 * 512, (16, 256))
                fw.dma("sp", lambda e: e.dma_start(out=ktv[:, :, 0:nkeys], in_=ktT[s, 2 * h:2 * h + 2, :, 0:nkeys].rearrange("m d t -> d m t")),
                       s_kv[(h % 2) * 2], reads=[b_kt[s]], writes=ktb)
                fw.dma("sp", lambda e: e.dma_start(out=vv[:, 0:nkb, :], in_=vsc[s, 0:nkeys, h * 256:(h + 1) * 256].rearrange("(kb p) e -> p kb e", p=128)),
                       s_kv[(h % 2) * 2 + 1], reads=[b_vs[s]], writes=vb)
                kblocks = []
                for kb in range(nkb):
                    jj = kb - 4 * ti
                    qoff = 0 if jj < 0 else 128 * jj
                    ktf = (lambda j, kb=kb: (ktv[:, j, kb * 128:(kb + 1) * 128], ktb))
                    kblocks.append((ktf, vv[:, kb, :], vb, 128, qoff, jj >= 0))
                attn_unit(h, 0, TT, kblocks)
            for h in range(NH):
                p_head(h)
        else:
            def s_head(b, h):
                par = (b * NH + h) % 2
                base = [(0, 8, 32), (40, 48, 56)][par]
                krv, krb = av(base[0] * 512, (16, 256))
                ktv, ktb = av(base[1] * 512, (2, 2048))
                vv, vb = av(base[2] * 512, (16, 256))
                fw.dma("pool", lambda e: e.dma_start(out=krv[:, 0:NPB, :], in_=ck[b, :, h * 256:(h + 1) * 256].rearrange("(kb p) e -> p kb e", p=128)),
                       s_kv[par * 2], writes=krb)
                fw.dma("pool", lambda e: e.dma_start(out=vv[:, 0:NPB, :], in_=cv[b, :, h * 256:(h + 1) * 256].rearrange("(kb p) e -> p kb e", p=128)),
                       s_kv[par * 2 + 1], writes=vb)
                gi_ = 0
                for j in range(2):
                    for g0 in range(0, NPB, 4):
                        k = ps_next()
                        nb_ = min(4, NPB - g0)

                        def burst(e, k=k, g0=g0, nb_=nb_, j=j):
                            for q in range(nb_):
                                i = e.matmul(psum[k][:, q * 128:(q + 1) * 128], lhsT=krv[:, g0 + q, j * 128:(j + 1) * 128], rhs=identb[:], start=True, stop=True)
                            return i
                        op("pe", burst, reads=list(krb) + [b_const], writes=[b_ps[k]])
                        if gi_ % 2 == 0:
                            op("act", lambda e, k=k, g0=g0, nb_=nb_, j=j: e.activation(out=ktv[:, j, g0 * 128:(g0 + nb_) * 128], in_=psum[k][:, 0:nb_ * 128], func=AF.Identity),
                               reads=[b_ps[k]], pwrites=ktb)
                        else:
                            op("dve", lambda e, k=k, g0=g0, nb_=nb_, j=j: e.tensor_copy(out=ktv[:, j, g0 * 128:(g0 + nb_) * 128], in_=psum[k][:, 0:nb_ * 128]),
                               reads=[b_ps[k]], pwrites=ktb)
                        gi_ += 1
                kblocks = []
                for kb in range(NPB):
                    ktf = (lambda j, kb=kb: (ktv[:, j, kb * 128:(kb + 1) * 128], ktb))
                    kblocks.append((ktf, vv[:, kb, :], vb, 128, 0, False))
                ktf = (lambda j: (ktn_v[:, 2 * h + j, b * SS:(b + 1) * SS], ktn_b))
                kblocks.append((ktf, vn_v[0:SS, b, h * 256:(h + 1) * 256], vn_b, SS, 0, False))
                attn_unit(h, b * SS, SS, kblocks)
            for b in range(NSEQ):
                for h in range(NH):
                    s_head(b, h)

        if pend_tail[0] is not None:
            pend_tail[0]()
            pend_tail[0] = None
        proj_A("at_out", range(4), 16, on, T, resid_add(T, sq=True))

        ffn(1)

        kf = ps_next()

        def burstf(e):
            for c in range(DC):
                i = e.matmul(psum[kf][:, 0:T], lhsT=onesb[:], rhs=slot(c)[:, 0:T], start=(c == 0), stop=(c == DC - 1))
            return i
        op("pe", burstf, reads=b_slot[0:DC] + [b_const], writes=[b_ps[kf]])
        op("act", lambda e: e.activation(out=rstd_sb[:, 0:T], in_=psum[kf][:, 0:T], func=AF.Sqrt, scale=1.0 / D, bias=EPS),
           reads=[b_ps[kf]], writes=[b_rstd])
        op("dve", lambda e: e.reciprocal(out=rstd_sb[:, 0:T], in_=rstd_sb[:, 0:T]), writes=[b_rstd])

        def final_cg(cg):
            yts = []
            for cl in range(4):
                c = cg * 4 + cl
                yt = TP.get()
                op("dve", lambda e, c=c, yt=yt: e.scalar_tensor_tensor(out=yt.a[:, 0:T], in0=xT[:, c, 0:T], scalar=pcol(P_FIN, c), in1=rstd_sb[:, 0:T],
                                                                     op0=ALU.mult, op1=ALU.mult), reads=[b_xT[c], b_rstd, b_prm], writes=[yt.b])
                yts.append(yt)
            for bi, (c0, ntok) in enumerate(tokblocks):
                k = ps_next()

                def burst(e, k=k, c0=c0, ntok=ntok):
                    for cl in range(4):
                        i = e.transpose(out=psum[k][0:ntok, cl * 128:(cl + 1) * 128], in_=yts[cl].a[:, c0:c0 + ntok], identity=ident[:])
                    return i
                op("pe", burst, reads=[y.b for y in yts] + [b_const], writes=[b_ps[k]])
                ot = TP.get()
                op("act", lambda e, k=k, ntok=ntok, ot=ot: e.activation(out=ot.a[0:ntok, 0:512], in_=psum[k][0:ntok, :], func=AF.Identity),
                   reads=[b_ps[k]], writes=[ot.b])
                dst = out_rows(y_p, y_s, bi)[:, cg * 512:(cg + 1) * 512]
                fw.dma("sp", lambda e, ot=ot, ntok=ntok, dst=dst: e.dma_start(out=dst, in_=ot.a[0:ntok, 0:512]),
                       ot.s, reads=[ot.b])
        for cg in range(4):
            final_cg(cg)


    op("dve", lambda e: e.memset(stb[:], 0.0), writes=all_st)
    cnt = 0
    for s in range(NSEQ):
        for ti in range(NTILE):
            if n_ptiles is not None and cnt >= n_ptiles:
                continue
            tile("p", s, ti)
            cnt += 1
            pool_ok[0] = True
            cur_first[0] = False
    fw.dma("sp", lambda e: e.dma_start(out=st_p[:], in_=stb[:]), s_st, reads=all_st)
    if do_sample:
        pool_ok[0] = True
        fw.dma("sp", lambda e: e.dma_start(out=stb[:], in_=st_in[:]), s_st, writes=all_st)
        tile("s", 0, 0)
        fw.dma("sp", lambda e: e.dma_start(out=st_s[:], in_=stb[:]), s_st, reads=all_st)
    e = fw.E["sp"]
    for name, val in fw.dsem_n.items():
        if val > 0:
            fw._wait(e, ("d", name, val))


_NC_CACHE = {}


def _fm(v):
    v = np.asarray(v)
    lead = v.shape[:-1]
    c = v.shape[-1] // 128
    v = v.reshape(lead + (c, 128))
    return np.moveaxis(v, -1, 0)


def _pack_prm(i):
    prm = np.zeros((128, NPRM), np.float32)
    prm[:, P_RGN:P_RGN + 16] = _fm(i["rg_norm"][0])
    prm[:, P_RGCW:P_RGCW + 64] = np.moveaxis(_fm(i["rg_conv_w"][0]), 1, 2).reshape(128, 64)
    prm[:, P_RGCB:P_RGCB + 16] = _fm(i["rg_conv_b"][0])
    gb = np.asarray(i["rg_gate_b"][0]).reshape(16, 2, 128)
    prm[:, P_RGGB:P_RGGB + 32] = np.transpose(gb, (2, 0, 1)).reshape(128, 32)
    prm[:, P_LL:P_LL + 16] = _fm(i["rg_log_lambda"][0])
    prm[:, P_ATN:P_ATN + 16] = _fm(i["at_norm"][0])
    prm[:, P_LAM:P_LAM + 4] = np.asarray(i["at_lambda"][0]).T
    prm[:, P_SUB:P_SUB + 2] = _fm(i["at_subln"][0])
    prm[:, P_FFN:P_FFN + 32] = _fm(i["ffn_norm"]).reshape(128, 32)
    fcw = _fm(i["ffn_conv_w"])
    prm[:, P_FCW:P_FCW + 528] = np.transpose(fcw, (0, 1, 3, 2)).reshape(128, 528)
    prm[:, P_FCB:P_FCB + 176] = _fm(i["ffn_conv_b"]).reshape(128, 176)
    prm[:, P_FIN:P_FIN + 16] = _fm(i["final_norm"])
    return prm


def _pack_state(rgc, rgh, ffn):
    st = np.zeros((128, NST), np.float32)
    a = _fm(rgc)
    st[:, S_RGC:S_RGC + 96] = np.transpose(a, (0, 3, 1, 2)).reshape(128, 96)
    a = _fm(rgh)
    st[:, S_RGH:S_RGH + 32] = np.transpose(a, (0, 2, 1)).reshape(128, 32)
    a = _fm(ffn)
    st[:, S_FFN:S_FFN + 704] = np.transpose(a, (0, 1, 4, 2, 3)).reshape(128, 704)
    return st


def _unpack_state(st):
    a = st[:, S_RGC:S_RGC + 96].reshape(128, 16, 2, 3)
    rgc = np.transpose(a, (2, 3, 1, 0)).reshape(2, 3, D)
    a = st[:, S_RGH:S_RGH + 32].reshape(128, 16, 2)
    rgh = np.transpose(a, (2, 1, 0)).reshape(2, D)
    a = st[:, S_FFN:S_FFN + 704].reshape(128, 2, 88, 2, 2)
    ffn = np.transpose(a, (1, 3, 4, 2, 0)).reshape(2, 2, 2, 2 * DFF)
    return rgc, rgh, ffn


def run(inputs, n_cores, SEQ, PAST, trace=False):
    key = (SEQ, PAST)
    if key not in _NC_CACHE:
        _NC_CACHE[key] = build_nc(SEQ, PAST)
    nc = _NC_CACHE[key]
    i = {k: np.asarray(v) for k, v in inputs.items()}
    prm = _pack_prm(i)
    ident = np.eye(128, dtype=np.float32)
    shared = {
        "prm": prm, "ident": ident,
        "w_rg_in": np.ascontiguousarray(i["rg_w_in"][0]),
        "w_gate": np.ascontiguousarray(i["rg_gate_w"][0].reshape(DC * 128, 256)),
        "w_rg_out": np.ascontiguousarray(i["rg_w_out"][0]),
        "w_up0": np.ascontiguousarray(i["ffn_w_up"][0]), "w_up1": np.ascontiguousarray(i["ffn_w_up"][1]),
        "w_down0": np.ascontiguousarray(i["ffn_w_down"][0]), "w_down1": np.ascontiguousarray(i["ffn_w_down"][1]),
        "w_qkv": np.ascontiguousarray(i["at_w_qkv"][0]),
        "w_at_out": np.ascontiguousarray(i["at_w_out"][0]),
    }
    in_maps = []
    for c in range(n_cores):
        b0 = c * NSEQ
        m = dict(shared)
        m["xp"] = np.ascontiguousarray(i["x_prompt"][b0:b0 + NSEQ])
        m["xs"] = np.ascontiguousarray(i["x_sample"][b0:b0 + NSEQ])
        m["st_in"] = _pack_state(i["state_rglru_conv"][0, b0:b0 + NSEQ], i["state_rglru_h"][0, b0:b0 + NSEQ],
                                 i["state_ffn_conv"][:, b0:b0 + NSEQ])
        m["ck"] = np.ascontiguousarray(i["cache_attn_k"][0, b0:b0 + NSEQ].reshape(NSEQ, PAST, D))
        m["cv"] = np.ascontiguousarray(i["cache_attn_v"][0, b0:b0 + NSEQ].reshape(NSEQ, PAST, D))
        in_maps.append(m)
    res = run_bass_kernel_spmd(nc, in_maps, core_ids=list(range(n_cores)), trace=trace)
    R = res.results
    cat = lambda k: np.concatenate([r[k] for r in R], axis=0)
    y_p = cat("y_p")
    y_s = cat("y_s")
    k_p = cat("k_p").reshape(1, n_cores * NSEQ, SEQ, 2 * NH, HD)
    v_p = cat("v_p").reshape(1, n_cores * NSEQ, SEQ, NH, 2 * HD)
    k_s = cat("k_s").reshape(1, n_cores * NSEQ, SS, 2 * NH, HD)
    v_s = cat("v_s").reshape(1, n_cores * NSEQ, SS, NH, 2 * HD)
    sp = [_unpack_state(r["st_p"]) for r in R]
    ss = [_unpack_state(r["st_s"]) for r in R]
    rgc_p = np.concatenate([x[0] for x in sp], 0)[None]
    rgh_p = np.concatenate([x[1] for x in sp], 0)[None]
    ffn_p = np.concatenate([x[2] for x in sp], 1)
    rgc_s = np.concatenate([x[0] for x in ss], 0)[None]
    rgh_s = np.concatenate([x[1] for x in ss], 0)[None]
    ffn_s = np.concatenate([x[2] for x in ss], 1)
    outs = (y_p, y_s, rgc_p, rgh_p, k_p, v_p, ffn_p, rgc_s, rgh_s, k_s, v_s, ffn_s)
    outs = tuple(np.ascontiguousarray(o, dtype=np.float32) for o in outs)
    return outs, res


def kernel(**inputs):
    outs, _ = run(inputs, 8, 2048, 2048)
    return outs
```
